# Optimizing a Trainium2 kernel written in Bass

```python
import jax
import jax.numpy as jnp
from jax import lax
import numpy as np

D_MODEL = 2048
BATCH = 8
SEQ = 2048
DEPTH = 1

D_MIX = D_MODEL
RWKV_HEAD = 64
RWKV_WIDTH = D_MIX // 2
RWKV_HEADS = RWKV_WIDTH // RWKV_HEAD
DECAY_LORA = 64
AAA_LORA = 64
GATE_LORA = 160
GN_EPS = 64e-5
QK_NOPE = 128
QK_ROPE = 64
V_HEAD = 128
MLA_WIDTH = D_MIX - RWKV_WIDTH
MLA_HEADS = MLA_WIDTH // V_HEAD
Q_LORA = 512
KV_LORA = 512
ROPE_THETA = 10000.0
Q_BLOCK = 128
D_FF = 5632
CONV_W = 3
NORM_EPS = 1e-6
NEG_INF = -1e30

IN_SPLITS = (RWKV_WIDTH, RWKV_WIDTH, RWKV_WIDTH, DECAY_LORA, AAA_LORA, GATE_LORA,
             Q_LORA, KV_LORA, QK_ROPE)
D_IN = sum(IN_SPLITS)
RWKV_SHIFT_DIM = 3 * RWKV_WIDTH + DECAY_LORA + AAA_LORA + GATE_LORA

kernel_name = 'hybrid_rwkv7_mla_convglu'


def _split(t, sizes):
    idx = np.cumsum(sizes)[:-1].tolist()
    return jnp.split(t, idx, axis=-1)


def rms_norm(x, g):
    xf = x.astype(jnp.float32)
    y = xf * lax.rsqrt(jnp.mean(xf * xf, axis=-1, keepdims=True) + NORM_EPS)
    return (y * g.astype(jnp.float32)).astype(x.dtype)


def token_shift(h, mu):
    prev = jnp.pad(h, ((0, 0), (1, 0), (0, 0)))[:, :-1]
    return h + (prev - h) * mu


def apply_rope(t, positions):
    half = t.shape[-1] // 2
    inv_freq = ROPE_THETA ** (-jnp.arange(half, dtype=jnp.float32) / half)
    ang = positions.astype(jnp.float32)[..., None] * inv_freq
    ang = ang.reshape(ang.shape[:2] + (1,) * (t.ndim - 3) + (half,))
    cos, sin = jnp.cos(ang), jnp.sin(ang)
    tf = t.astype(jnp.float32)
    t1, t2 = tf[..., :half], tf[..., half:]
    return jnp.concatenate([t1 * cos - t2 * sin, t2 * cos + t1 * sin], axis=-1).astype(t.dtype)


def rwkv7_scan(r, w, k, v, a, b):
    bsz, _, h, n = r.shape

    def step(state, inp):
        r_t, w_t, k_t, v_t, a_t, b_t = inp
        sa = jnp.einsum('bhvk,bhk->bhv', state, a_t)
        state = (state * w_t[:, :, None, :]
                 + sa[..., None] * b_t[:, :, None, :]
                 + v_t[..., None] * k_t[:, :, None, :])
        return state, jnp.einsum('bhvk,bhk->bhv', state, r_t)

    init = jnp.zeros((bsz, h, n, n), jnp.float32)
    xs = tuple(jnp.moveaxis(t, 1, 0) for t in (r, w, k, v, a, b))
    _, out = lax.scan(step, init, xs)
    return jnp.moveaxis(out, 0, 1)


def rwkv7_mixer(h_r, h_k, h_v, h_w, h_a, h_g, w0, w2, a0, a2, g2, k_k, k_a, r_k, gn_w, gn_b):
    f32 = jnp.float32
    bsz, s, c = h_r.shape
    hd = (bsz, s, RWKV_HEADS, RWKV_HEAD)
    r = h_r.astype(f32)
    k = h_k.astype(f32)
    v = h_v.astype(f32)
    w_log = -jax.nn.softplus(-(w0.astype(f32) + jnp.tanh(h_w.astype(f32)) @ w2.astype(f32))) - 0.5
    decay = jnp.exp(-jnp.exp(w_log))
    a = jax.nn.sigmoid(a0.astype(f32) + h_a.astype(f32) @ a2.astype(f32))
    g = jax.nn.sigmoid(h_g.astype(f32)) @ g2.astype(f32)
    kk = (k * k_k).reshape(hd)
    kk = kk * lax.rsqrt(jnp.maximum(jnp.sum(kk * kk, axis=-1, keepdims=True), 1e-24))
    k = k * (1.0 + (a - 1.0) * k_a)
    r, k, v, decay, a = (t.reshape(hd) for t in (r, k, v, decay, a))
    y = rwkv7_scan(r, decay, k, v, -kk, kk * a)
    mu = jnp.mean(y, axis=-1, keepdims=True)
    var = jnp.mean(jnp.square(y - mu), axis=-1, keepdims=True)
    y = ((y - mu) * lax.rsqrt(var + GN_EPS)).reshape(bsz, s, c) * gn_w + gn_b
    bonus = jnp.sum(r * k * r_k, axis=-1, keepdims=True) * v
    return (y + bonus.reshape(bsz, s, c)) * g


def mla_mixer(c_q, c_kv, k_pe, positions, q_norm_g, w_uq, kv_norm_g, w_ukv):
    f32 = jnp.float32
    bsz, s, _ = c_q.shape
    q = (rms_norm(c_q, q_norm_g) @ w_uq).reshape(bsz, s, MLA_HEADS, QK_NOPE + QK_ROPE)
    q_nope = q[..., :QK_NOPE]
    q_pe = apply_rope(q[..., QK_NOPE:], positions)
    kv = (rms_norm(c_kv, kv_norm_g) @ w_ukv).reshape(bsz, s, MLA_HEADS, QK_NOPE + V_HEAD)
    k_nope, v = kv[..., :QK_NOPE], kv[..., QK_NOPE:]
    k_pe = apply_rope(k_pe, positions)
    scale = (QK_NOPE + QK_ROPE) ** -0.5
    outs = []
    for i in range(s // Q_BLOCK):
        q0, q1 = i * Q_BLOCK, (i + 1) * Q_BLOCK
        sc = (jnp.einsum('bqhd,bkhd->bhqk', q_nope[:, q0:q1], k_nope[:, :q1])
              + jnp.einsum('bqhr,bkr->bhqk', q_pe[:, q0:q1], k_pe[:, :q1])).astype(f32) * scale
        causal = (q0 + jnp.arange(Q_BLOCK))[:, None] >= jnp.arange(q1)[None, :]
        p = jax.nn.softmax(jnp.where(causal, sc, NEG_INF), axis=-1)
        outs.append(jnp.einsum('bhqk,bkhd->bqhd', p.astype(v.dtype), v[:, :q1]))
    return jnp.concatenate(outs, axis=1).reshape(bsz, s, MLA_WIDTH)


def conv_glu_ffn(h, w_gate, w_up, conv_w, conv_b, w_down):
    gate = h @ w_gate
    gate = lax.conv_general_dilated(
        gate, conv_w[:, None, :].astype(gate.dtype), window_strides=(1,),
        padding=[(CONV_W - 1, 0)], dimension_numbers=('NWC', 'WIO', 'NWC'),
        feature_group_count=D_FF) + conv_b
    return (jax.nn.silu(gate) * (h @ w_up)) @ w_down


def setup_inputs(seed: int = 0) -> dict:
    key = jax.random.key(seed)
    ks = jax.random.split(key, 32)
    L = DEPTH

    def nrm(k, shape, scale):
        return jax.random.normal(k, shape, jnp.float32) * scale

    def gain(k, shape):
        return 1.0 + nrm(k, shape, 0.05)

    x = nrm(ks[0], (BATCH, SEQ, D_MODEL), 1.0)
    offset = jax.random.randint(ks[1], (BATCH, 1), 0, 4096, dtype=jnp.int32)
    positions = offset + jnp.arange(SEQ, dtype=jnp.int32)[None, :]
    return {
        'x': x,
        'positions': positions,
        'attn_norm_g': gain(ks[2], (L, D_MODEL)),
        'w_in': nrm(ks[3], (L, D_MODEL, D_IN), D_MODEL ** -0.5),
        'rwkv_mu': jax.random.uniform(ks[4], (L, RWKV_SHIFT_DIM), jnp.float32),
        'rwkv_w0': jax.random.uniform(ks[5], (L, RWKV_WIDTH), jnp.float32, -6.0, -1.0),
        'rwkv_w2': nrm(ks[6], (L, DECAY_LORA, RWKV_WIDTH), 0.1 * DECAY_LORA ** -0.5),
        'rwkv_a0': nrm(ks[7], (L, RWKV_WIDTH), 0.5),
        'rwkv_a2': nrm(ks[8], (L, AAA_LORA, RWKV_WIDTH), 0.5 * AAA_LORA ** -0.5),
        'rwkv_g2': nrm(ks[9], (L, GATE_LORA, RWKV_WIDTH), GATE_LORA ** -0.5),
        'rwkv_k_k': 0.85 + nrm(ks[10], (L, RWKV_WIDTH), 0.05),
        'rwkv_k_a': gain(ks[11], (L, RWKV_WIDTH)),
        'rwkv_r_k': nrm(ks[12], (L, RWKV_HEADS, RWKV_HEAD), 0.1),
        'rwkv_gn_w': gain(ks[13], (L, RWKV_WIDTH)),
        'rwkv_gn_b': nrm(ks[14], (L, RWKV_WIDTH), 0.02),
        'mla_q_norm_g': gain(ks[15], (L, Q_LORA)),
        'mla_w_uq': nrm(ks[16], (L, Q_LORA, MLA_HEADS * (QK_NOPE + QK_ROPE)), Q_LORA ** -0.5),
        'mla_kv_norm_g': gain(ks[17], (L, KV_LORA)),
        'mla_w_ukv': nrm(ks[18], (L, KV_LORA, MLA_HEADS * (QK_NOPE + V_HEAD)), KV_LORA ** -0.5),
        'w_out': nrm(ks[19], (L, D_MIX, D_MODEL), D_MIX ** -0.5),
        'ffn_norm_g': gain(ks[20], (L, D_MODEL)),
        'ffn_w_gate': nrm(ks[21], (L, D_MODEL, D_FF), D_MODEL ** -0.5),
        'ffn_w_up': nrm(ks[22], (L, D_MODEL, D_FF), D_MODEL ** -0.5),
        'ffn_conv_w': nrm(ks[23], (L, CONV_W, D_FF), CONV_W ** -0.5),
        'ffn_conv_b': nrm(ks[24], (L, D_FF), 0.02),
        'ffn_w_down': nrm(ks[25], (L, D_FF, D_MODEL), D_FF ** -0.5),
        'final_norm_g': gain(ks[26], (D_MODEL,)),
    }


def reference(x, positions, attn_norm_g, w_in, rwkv_mu, rwkv_w0, rwkv_w2, rwkv_a0, rwkv_a2,
              rwkv_g2, rwkv_k_k, rwkv_k_a, rwkv_r_k, rwkv_gn_w, rwkv_gn_b, mla_q_norm_g,
              mla_w_uq, mla_kv_norm_g, mla_w_ukv, w_out, ffn_norm_g, ffn_w_gate, ffn_w_up,
              ffn_conv_w, ffn_conv_b, ffn_w_down, final_norm_g):
    for l in range(DEPTH):
        h = rms_norm(x, attn_norm_g[l])
        proj = h @ w_in[l]
        shifted = token_shift(proj[..., :RWKV_SHIFT_DIM], rwkv_mu[l])
        h_r, h_k, h_v, h_w, h_a, h_g = _split(shifted, IN_SPLITS[:6])
        c_q, c_kv, k_pe = _split(proj[..., RWKV_SHIFT_DIM:], IN_SPLITS[6:])
        y_rwkv = rwkv7_mixer(h_r, h_k, h_v, h_w, h_a, h_g, rwkv_w0[l], rwkv_w2[l], rwkv_a0[l],
                             rwkv_a2[l], rwkv_g2[l], rwkv_k_k[l], rwkv_k_a[l], rwkv_r_k[l],
                             rwkv_gn_w[l], rwkv_gn_b[l])
        y_mla = mla_mixer(c_q, c_kv, k_pe, positions, mla_q_norm_g[l], mla_w_uq[l],
                          mla_kv_norm_g[l], mla_w_ukv[l])
        y = jnp.concatenate([y_rwkv.astype(x.dtype), y_mla.astype(x.dtype)], axis=-1)
        x = x + y @ w_out[l]
        h = rms_norm(x, ffn_norm_g[l])
        x = x + conv_glu_ffn(h, ffn_w_gate[l], ffn_w_up[l], ffn_conv_w[l], ffn_conv_b[l],
                             ffn_w_down[l]).astype(x.dtype)
    return rms_norm(x, final_norm_g)
```

```python
import math
from contextlib import ExitStack

import numpy as np
import concourse.bass as bass
import concourse.mybir as mybir
from concourse.bass_utils import run_bass_kernel_spmd

F32 = mybir.dt.float32
BF16 = mybir.dt.bfloat16
I32 = mybir.dt.int32
ALU = mybir.AluOpType
AF = mybir.ActivationFunctionType
AX = mybir.AxisListType

S = 2048
D = 2048
T = 1024
NH = S // T
NTT = T // 128
DC = D // 128
DIN = 4448
DFF = 5632
NFC = DFF // 128
HN = 64
NHEAD = 16
CH = 128
TWO_PI = 2.0 * math.pi

ENGS = ("pe", "act", "dve", "pool", "sp")


class Op:
    __slots__ = ("eng", "fn", "deps", "dma", "ms", "need")

    def __init__(self, eng, fn, dma):
        self.eng = eng
        self.fn = fn
        self.deps = []
        self.dma = dma
        self.ms = None
        self.need = False


class Prog:
    def __init__(self):
        self.q = {e: [] for e in ENGS}
        self.lastw = {}
        self.readers = {}
        self.last_dma = {}

    def op(self, eng, fn, r=(), w=(), dma=None):
        o = Op(eng, fn, dma)
        deps = set()
        isb = lambda k: isinstance(k, tuple) and k and k[0] == "bank"
        w = list(w) + [k for k in r if isb(k)]
        r = [k for k in r if not isb(k)]
        for k in r:
            lw = self.lastw.get(k)
            if lw is not None:
                deps.add(lw)
        for k in w:
            lw = self.lastw.get(k)
            if lw is not None:
                deps.add(lw)
            for rd in self.readers.get(k, ()):
                deps.add(rd)
        for d in deps:
            if d.eng == "pe" and eng == "pe" and d.dma is None and dma is None:
                continue
            o.deps.append(d)
            d.need = True
        for k in w:
            self.lastw[k] = o
            self.readers[k] = []
        for k in r:
            self.readers.setdefault(k, []).append(o)
        self.q[eng].append(o)
        if dma is not None:
            self.last_dma[dma] = o
        return o

    def mark(self, name):
        if not hasattr(self, "marks"):
            self.marks = []
        self.marks.append((name, {e: len(self.q[e]) for e in ENGS}))

    def barrier(self):
        lasts = [self.q[e][-1] for e in ENGS if self.q[e]]
        dmas = list(self.last_dma.values())
        self.last_dma = {}
        for e in ENGS:
            o = Op(e, None, None)
            for d in lasts + dmas:
                if d.fn is None:
                    continue
                if d.eng == "pe" and e == "pe" and d.dma is None:
                    continue
                o.deps.append(d)
                d.need = True
            self.q[e].append(o)
        self.lastw = {}
        self.readers = {}

    def emit(self, nc, stack):
        sems = {e: stack.enter_context(nc.semaphore("s_" + e)) for e in ENGS}
        dsems = {}
        cnt = {e: 0 for e in ENGS}
        dcnt = {}
        for e in ENGS:
            for o in self.q[e]:
                if o.dma is not None:
                    if o.dma not in dsems:
                        dsems[o.dma] = stack.enter_context(nc.semaphore("d_" + o.dma))
                        dcnt[o.dma] = 0
                    dcnt[o.dma] += 16
                    o.ms = dcnt[o.dma]
                elif o.need and o.fn is not None:
                    cnt[e] += 1
                    o.ms = cnt[e]
        self.nsem = len(sems) + len(dsems)
        self.counts = dict(cnt)
        block = stack.enter_context(nc.Block())
        prog = self

        def run(e, eng):
            waited = {}
            for o in prog.q[e]:
                for d in o.deps:
                    if d.dma is not None:
                        s, v = dsems[d.dma], d.ms
                    else:
                        s, v = sems[d.eng], d.ms
                    if waited.get(s.num, 0) >= v:
                        continue
                    eng.wait_ge(s, v)
                    waited[s.num] = v
                if o.fn is None:
                    continue
                ins = o.fn(eng)
                if o.dma is not None:
                    ins.then_inc(dsems[o.dma], 16)
                elif o.need:
                    ins.then_inc(sems[e], 1)

        @block.tensor
        def _(eng):
            run("pe", eng)

        @block.scalar
        def _(eng):
            run("act", eng)

        @block.vector
        def _(eng):
            run("dve", eng)

        @block.gpsimd
        def _(eng):
            run("pool", eng)

        @block.sync
        def _(eng):
            run("sp", eng)


RW_SEGS = [(j * 128, 128) for j in range(24)] + [(3072, 128), (3200, 128), (3328, 32)]
CQ0, CKV0, KPE0 = 3360, 3872, 4384
ARENA_KIB = 165
C_MU, C_W0, C_A0, C_KK, C_KA, C_RK, C_GNW, C_GNB, C_GQ, C_GKV, C_CB, C_CW = 0, 27, 35, 43, 51, 59, 67, 75, 83, 87, 91, 135
C_INVF, C_OFF, C_NW0, C_NA0, C_EPS, C_MHALF, C_ONE, C_GNEPS, C_EPS24 = 267, 268, 269, 277, 285, 286, 287, 288, 289
NPC_HOST = 269
NPC = 292


class K:
    def __init__(self, stage=99, taps=()):
        self.stage = stage
        self.taps = set(taps)
        self.nc = bass.Bass("TRN2", target_bir_lowering=False)
        self.P = Prog()
        self.din = {}
        self.rr = 0
        self.bank_rr = 0

    def dram_in(self, name, shape, dt=F32):
        t = self.nc.dram_tensor(name, list(shape), dt, kind="ExternalInput").ap()
        self.din[name] = t
        return t

    def sb(self, st, name, shape, dt):
        return st.enter_context(self.nc.sbuf_tensor(name, list(shape), dt))

    def cv(self, off_kib, shape, dt):
        esz = 2 if dt == BF16 else 4
        n = 1
        for d in shape[1:]:
            n *= d
        o = int(round(off_kib * 1024)) // 2
        assert o * 2 + n * esz <= ARENA_KIB * 1024, (off_kib, shape)
        ap = self.arena[0:shape[0], o:o + n * esz // 2]
        if dt != BF16:
            ap = ap.bitcast(dt)
        if len(shape) == 3:
            ap = ap.rearrange("p (a b) -> p a b", a=shape[1])
        return ap

    def tap(self, name, src_ap, shape, keys):
        if name not in self.taps:
            return
        o = self.nc.dram_tensor("tap_" + name, list(shape), src_ap.dtype, kind="ExternalOutput").ap()
        if len(shape) == 3:
            for a in range(shape[1]):
                self.P.op("sp", lambda e, a=a: e.dma_start(out=o[:, a, :], in_=src_ap[:, a, :]), r=keys, dma="tap_" + name)
        else:
            self.P.op("sp", lambda e: e.dma_start(out=o, in_=src_ap), r=keys, dma="tap_" + name)

    def load(self, dst_ap, src_ap, wkeys, grp, eng="sp"):
        return self.P.op(eng, lambda e: e.dma_start(out=dst_ap, in_=src_ap), w=wkeys, dma=grp)

    def loadw(self, dst3, W, r0, c0, ncols, KC, wkeys, grp, step=4):
        for k0 in range(0, KC, step):
            k1 = min(KC, k0 + step)
            src = W[r0 + k0 * 128:r0 + k1 * 128, c0:c0 + ncols].rearrange("(kc p) c -> p kc c", p=128)
            self.load(dst3[:, k0:k1, 0:ncols], src, wkeys, grp, eng="pool")

    def evac_eng(self):
        self.rr += 1
        return "act" if self.rr % 2 else "dve"

    def copy(self, eng, out, in_, r, w):
        if eng == "act":
            return self.P.op("act", lambda e: e.copy(out=out, in_=in_), r=r, w=w)
        return self.P.op(eng, lambda e: e.tensor_copy(out=out, in_=in_), r=r, w=w)

    def next_bank(self, lo=4, hi=8):
        b = lo + (self.bank_rr % (hi - lo))
        self.bank_rr += 1
        return b

    def col(self, c, p0=0, p1=128):
        return self.pc[p0:p1, c:c + 1]

    def build(self):
        nc, P = self.nc, self.P
        self.x = self.dram_in("x", [S, D])
        self.out = nc.dram_tensor("out", [S, D], F32, kind="ExternalOutput").ap()
        with ExitStack() as st:
            self.setup_consts(st)
            for hf in range(NH):
                if self.stage <= 0:
                    break
                self.half(hf)
                if self.stage < 99:
                    break
            P.barrier()
            P.emit(nc, st)
        return nc

    def setup_consts(self, st):
        nc, P = self.nc, self.P
        sb = lambda n, s, d: self.sb(st, n, s, d)
        self.psS = st.enter_context(nc.psum_tensor("psS", [128, 2048], F32))
        self.psX = [st.enter_context(nc.psum_tensor("psX%d" % i, [128, 512], F32)) for i in range(4)]
        self.bank = [self.psS[:, i * 512:(i + 1) * 512] for i in range(4)] + [p[:] for p in self.psX]
        self.bkey = [("bank", i) for i in range(8)]
        cf = self.dram_in("c_f32", [128, 7 * 128])
        self.cF = sb("cF", [128, 5 * 128], F32)
        self.cB = sb("cB", [128, 7 * 128], BF16)
        self.load(self.cF[:], cf[:, 0:640], ["cF"], "c0")
        self.load(self.cB[:], cf[:, :], ["cB"], "c1", eng="pool")
        self.identF, self.iu, self.iue, self.il, self.onesbdF = [self.cF[:, i * 128:(i + 1) * 128] for i in range(5)]
        self.identB = self.cB[:, 0:128]
        self.onesbdB = self.cB[:, 512:640]
        self.cmaskB = self.cB[:, 640:768]
        self.sel2B = self.cB[:, 768:896]
        self.onesB = sb("onesB", [128, 128], BF16)
        P.op("dve", lambda e: e.memset(self.onesB[:], 1.0), w=["onesB"])
        self.onesF = sb("onesF", [128, 128], F32)
        P.op("dve", lambda e: e.memset(self.onesF[:], 1.0), w=["onesF"])
        pc = self.dram_in("pcols", [128, NPC_HOST])
        self.pc = sb("pc", [128, NPC], F32)
        self.load(self.pc[:, 0:NPC_HOST], pc[:, :], ["pc"], "c2")
        P.op("dve", lambda e: e.tensor_scalar(out=self.pc[:, C_NW0:C_NW0 + 16], in0=self.pc[:, C_W0:C_W0 + 16], scalar1=-1.0, scalar2=None, op0=ALU.mult), r=["pc"], w=["pc"])
        for c, v in ((C_EPS, 1e-6), (C_MHALF, -0.5), (C_ONE, 1.0), (C_GNEPS, 64e-5), (C_EPS24, 1e-24)):
            P.op("dve", lambda e, c=c, v=v: e.memset(self.pc[:, c:c + 1], v), r=["pc"], w=["pc"])
        self.gbd = self.dram_in("gains_b", [3, 128, D])
        self.gA = sb("gA", [128, D], F32)
        self.carry = sb("carry", [128, 27], F32)
        P.op("dve", lambda e: e.memset(self.carry[:], 0.0), w=["carry"])
        self.gcarry = sb("gcarry", [128, NFC, 2], F32)
        P.op("dve", lambda e: e.memset(self.gcarry[:], 0.0), w=["gcarry"])
        self.STp = sb("STp", [128, 8, 128], F32)
        P.op("dve", lambda e: e.memset(self.STp[:], 0.0), w=["STp"])
        self.ckvn = sb("ckvn", [128, 4, S], BF16)
        self.kpe2 = sb("kpe2", [128, S], BF16)
        self.tab = sb("ropetab", [128, S], BF16)
        self.w2a2_d = self.dram_in("w2a2", [128, 1024])
        self.g2_d = self.dram_in("g2", [160, 1024])
        self.w_in = self.dram_in("w_in", [D, DIN])
        self.w_uq = self.dram_in("w_uq", [512, 1536])
        self.w_ukv = self.dram_in("w_ukv", [512, 2048])
        self.w_out = self.dram_in("w_out", [D, D])
        self.w_gate = self.dram_in("w_gate", [D, DFF])
        self.w_up = self.dram_in("w_up", [D, DFF])
        self.w_down = self.dram_in("w_down", [DFF, D])
        self.arena = sb("arena", [128, ARENA_KIB * 512], BF16)
        self.rope_tables()

    def rope_tables(self):
        P = self.P
        posb = self.dram_in("pos_b", [128, S], I32)
        pi_ = self.cv(0, [128, S], I32)
        ang = self.cv(8, [128, S], F32)
        t1 = self.cv(16, [128, S], F32)
        ki = self.cv(24, [128, S], I32)
        kf = self.cv(32, [128, S], F32)
        self.load(pi_, posb[:, :], ["pos_i"], "c7")
        P.op("dve", lambda e: e.tensor_copy(out=ang, in_=pi_), r=["pos_i"], w=["ang"])
        P.op("dve", lambda e: e.tensor_scalar(out=ang, in0=ang, scalar1=self.col(C_INVF), scalar2=None, op0=ALU.mult), r=["ang", "pc"], w=["ang"])
        C1 = 6.28125
        C2 = TWO_PI - C1
        P.op("dve", lambda e: e.tensor_scalar(out=t1, in0=ang, scalar1=self.col(C_OFF), scalar2=None, op0=ALU.add), r=["ang", "pc"], w=["rt1"])
        P.op("dve", lambda e: e.tensor_scalar(out=kf, in0=t1, scalar1=1.0 / TWO_PI, scalar2=None, op0=ALU.mult), r=["rt1"], w=["rkf"])
        P.op("dve", lambda e: e.tensor_copy(out=ki, in_=kf), r=["rkf"], w=["rki"])
        P.op("dve", lambda e: e.tensor_copy(out=kf, in_=ki), r=["rki"], w=["rkf"])
        P.op("dve", lambda e: e.scalar_tensor_tensor(out=t1, in0=kf, scalar=-C1, in1=t1, op0=ALU.mult, op1=ALU.add), r=["rkf", "rt1"], w=["rt1"])
        P.op("dve", lambda e: e.scalar_tensor_tensor(out=t1, in0=kf, scalar=-C2, in1=t1, op0=ALU.mult, op1=ALU.add), r=["rkf", "rt1"], w=["rt1"])
        P.op("dve", lambda e: e.tensor_scalar(out=kf, in0=t1, scalar1=math.pi, scalar2=-TWO_PI, op0=ALU.is_gt, op1=ALU.mult), r=["rt1"], w=["rkf"])
        P.op("dve", lambda e: e.tensor_tensor(out=t1, in0=t1, in1=kf, op=ALU.add), r=["rkf", "rt1"], w=["rt1"])
        P.op("dve", lambda e: e.tensor_scalar(out=kf, in0=t1, scalar1=-math.pi, scalar2=TWO_PI, op0=ALU.is_lt, op1=ALU.mult), r=["rt1"], w=["rkf"])
        P.op("dve", lambda e: e.tensor_tensor(out=t1, in0=t1, in1=kf, op=ALU.add), r=["rkf", "rt1"], w=["rt1"])
        P.op("dve", lambda e: e.tensor_scalar(out=t1, in0=t1, scalar1=-3.14159, scalar2=3.14159, op0=ALU.max, op1=ALU.min), r=["rt1"], w=["rt1"])
        P.op("act", lambda e: e.activation(out=self.tab[:], in_=t1, func=AF.Sin), r=["rt1"], w=["tab"])
        self.tap("tab", self.tab[:], [128, S], ["tab"])
        P.barrier()

    def norm_transpose(self, get_src, gidx, dstT, tag, off_kib):
        P = self.P
        hb = [self.cv(off_kib + 4 * i, [128, D], BF16) for i in range(2)]
        st_ = self.cv(off_kib + 8, [128, 4 * NTT], F32)
        self.load(self.gA[:], self.gbd[gidx], ["gA"], "gA")
        P.op("dve", lambda e: e.memset(st_, 0.0), w=[tag + "stat"])
        for tt in range(NTT):
            src, skeys = get_src(tt)
            c = 4 * tt
            h = hb[tt % 2]
            hk = (tag + "hb", tt % 2)
            P.op("act", lambda e, src=src, c=c, h=h: e.activation(out=h, in_=src, func=AF.Square, scale=float(D) ** -0.5, accum_out=st_[:, c:c + 1]),
                 r=skeys + [tag + "stat"], w=[hk, (tag + "st", tt)])
            P.op("act", lambda e, c=c: e.activation(out=st_[:, c + 2:c + 3], in_=st_[:, c:c + 1], func=AF.Sqrt, bias=self.col(C_EPS), scale=1.0), r=[(tag + "st", tt), "pc"], w=[(tag + "st2", tt)])
            P.op("dve", lambda e, c=c: e.reciprocal(out=st_[:, c + 3:c + 4], in_=st_[:, c + 2:c + 3]), r=[(tag + "st2", tt)], w=[(tag + "st3", tt)])
            P.op("dve", lambda e, src=src, c=c, h=h: e.scalar_tensor_tensor(out=h, in0=src, scalar=st_[:, c + 3:c + 4], in1=self.gA[:], op0=ALU.mult, op1=ALU.mult),
                 r=skeys + [(tag + "st3", tt), "gA"], w=[hk])
            import os
            if os.environ.get("K_DBG") == "1":
                continue
            for g in range(2):
                bi = self.next_bank()
                bk = self.bank[bi].bitcast(BF16)
                for j in range(8):
                    dc = g * 8 + j
                    P.op("pe", lambda e, bk=bk, j=j, dc=dc, h=h: e.transpose(out=bk[:, j * 128:(j + 1) * 128], in_=h[:, dc * 128:(dc + 1) * 128], identity=self.identB),
                         r=[hk, "cB"], w=[self.bkey[bi]])
                if os.environ.get("K_DBG") == "2":
                    continue
                self.copy(self.evac_eng(), dstT[:, g * 8:(g + 1) * 8, tt * 128:(tt + 1) * 128], bk.rearrange("p (j t) -> p j t", j=8),
                          r=[self.bkey[bi]], w=[(tag + "T", tt)])

    def proj_fm(self, W, segs, src, KC, srckeys, tgs, evac, wtag, wbufs, preload=None, wcols=512):
        P = self.P
        groups = []
        for si, (c0, n) in enumerate(segs):
            if groups and preload is None and groups[-1][1] == c0 and (c0 + n - groups[-1][0]) <= wcols:
                groups[-1][1] = c0 + n
                groups[-1][2].append((si, c0, n))
            else:
                groups.append([c0, c0 + n, [(si, c0, n)]])
        for gidx, (g0, g1, subs) in enumerate(groups):
            slot = gidx % len(wbufs)
            wt = wbufs[slot]
            wk = (wtag, slot)
            if preload is None:
                self.loadw(wt, W, 0, g0, g1 - g0, KC, [wk], "%s%d" % (wtag, slot), step=4)
            else:
                preload(subs[0][0], wt, wk)
            for (si, c0, n) in subs:
                off = c0 - g0
                for gi, (t0, nt) in enumerate(tgs):
                    bi = self.next_bank()
                    for kc in range(KC):
                        P.op("pe", lambda e, bi=bi, kc=kc, wt=wt, n=n, t0=t0, nt=nt, off=off: e.matmul(self.bank[bi][0:n, 0:nt], lhsT=wt[:, kc, off:off + n], rhs=src[:, kc, t0:t0 + nt], start=(kc == 0), stop=(kc == KC - 1)),
                             r=[wk] + srckeys, w=[self.bkey[bi]])
                    evac(si, (c0, n), gi, (t0, nt), bi)

    def half(self, hf):
        P = self.P
        x = self.x
        self.y = self.cv(0, [128, DC, T], BF16)
        self.rkv = self.cv(32, [128, 24, T], BF16)
        self.lora = self.cv(80, [128, 3, T], BF16)
        self.cqn = self.cv(86, [128, 4, T], BF16)
        hT = self.cv(94, [128, DC, T], BF16)
        wbufs = [self.cv(126 + 16 * i, [128, DC, 512], BF16) for i in range(2)]
        xt = [self.cv(126 + 8 * i, [128, D], F32) for i in range(2)]

        def get_src(tt):
            b = tt % 2
            self.load(xt[b], x[hf * T + tt * 128: hf * T + (tt + 1) * 128, :], [("xt", b)], "xt%d" % b)
            return xt[b], [("xt", b)]

        P.mark("h%d start" % hf)
        self.norm_transpose(get_src, 0, hT, "A", 142)
        P.mark("h%d normA done" % hf)
        hTkeys = [("AT", tt) for tt in range(NTT)]
        self.tap("hT%d" % hf, hT, [128, DC, T], hTkeys)
        if self.stage <= 1:
            return
        P.barrier()
        tgs = [(0, 512), (512, 512)]
        self.mla_proj(hf, hT, hTkeys, wbufs, tgs)
        P.mark("h%d mla_proj done" % hf)
        import os
        if os.environ.get("K_DBG") not in ("3", "4"):
            self.rwkv_proj(hf, hT, hTkeys, wbufs, tgs)
        P.barrier()
        P.mark("h%d rwkv_proj done" % hf)
        if self.stage <= 2:
            return
        self.mla_attn(hf)
        P.barrier()
        P.mark("h%d mla_attn done" % hf)
        if self.stage <= 3:
            return
        self.rwkv_chunks(hf)
        P.barrier()
        P.mark("h%d rwkv_chunks done" % hf)
        if self.stage <= 4:
            return
        self.ffn_block(hf)
        P.barrier()
        P.mark("h%d ffn_block done" % hf)

    def mla_proj(self, hf, hT, hTkeys, wbufs, tgs):
        import os
        P = self.P
        sq = self.cv(0, [128, 4, T], BF16)
        rstd_b = self.cv(8, [128, 512], F32)
        prodk = self.cv(10, [128, 512], BF16)
        for which, c0, dst, toff, gcol in (("q", CQ0, self.cqn, 0, C_GQ), ("kv", CKV0, self.ckvn, hf * T, C_GKV)):
            def evac(si, seg, gi, tg, bi, dst=dst, toff=toff, which=which):
                t0, nt = tg
                P.op("act", lambda e: e.activation(out=sq[:, si, t0:t0 + nt], in_=self.bank[bi][:, 0:nt], func=AF.Square), r=[self.bkey[bi]], w=[("sq", si, gi)])
                P.op("dve", lambda e: e.tensor_copy(out=dst[:, si, toff + t0:toff + t0 + nt], in_=self.bank[bi][:, 0:nt]), r=[self.bkey[bi]], w=[("c" + which, si, gi)])
            self.proj_fm(self.w_in, [(c0 + j * 128, 128) for j in range(4)], hT, DC, hTkeys, tgs, evac, "wb", wbufs)
            for gi, (t0, nt) in enumerate(tgs):
                if os.environ.get("K_DBG2") == "1":
                    continue
                bi = self.next_bank()
                for j in range(4):
                    P.op("pe", lambda e, j=j, bi=bi, t0=t0, nt=nt: e.matmul(self.bank[bi][:, 0:nt], lhsT=self.onesB[:], rhs=sq[:, j, t0:t0 + nt], start=(j == 0), stop=(j == 3)),
                         r=[("sq", j, gi), "onesB"], w=[self.bkey[bi]])
                P.op("act", lambda e, bi=bi: e.activation(out=rstd_b[:, 0:nt], in_=self.bank[bi][:, 0:nt], func=AF.Sqrt, bias=self.col(C_EPS), scale=1.0 / 512), r=[self.bkey[bi], "pc"], w=["rstd_b"])
                P.op("dve", lambda e: e.reciprocal(out=rstd_b[:, 0:nt], in_=rstd_b[:, 0:nt]), r=["rstd_b"], w=["rstd_b"])
                for j in range(4):
                    d = dst[:, j, toff + t0:toff + t0 + nt]
                    P.op("dve", lambda e, d=d, j=j, gcol=gcol: e.scalar_tensor_tensor(out=d, in0=d, scalar=self.col(gcol + j), in1=rstd_b[:, 0:nt], op0=ALU.mult, op1=ALU.mult),
                         r=[("c" + which, j, gi), "rstd_b", "pc"], w=[("c" + which, j, gi)])
        import os
        if os.environ.get("K_DBG") == "3":
            return
        def preload(si, wt, wk):
            W = self.w_in
            v = lambda a, b: W[:, a:b].rearrange("(kc p) c -> p kc c", p=128)
            self.load(wt[:, :, 0:64], v(KPE0, KPE0 + 64), [wk], "wpe0", eng="pool")
            self.load(wt[:, :, 64:96], v(KPE0 + 32, KPE0 + 64), [wk], "wpe1", eng="pool")
            self.load(wt[:, :, 96:128], v(KPE0, KPE0 + 32), [wk], "wpe2", eng="pool")
            P.op("dve", lambda e: e.tensor_scalar(out=wt[:, :, 64:96], in0=wt[:, :, 64:96], scalar1=-1.0, scalar2=None, op0=ALU.mult), r=[wk], w=[wk])

        def evac_pe(si, seg, gi, tg, bi):
            t0, nt = tg
            g0 = hf * T + t0
            P.op("dve", lambda e: e.tensor_tensor(out=prodk[:, 0:nt], in0=self.bank[bi][:, 0:nt], in1=self.tab[:, g0:g0 + nt], op=ALU.mult), r=[self.bkey[bi], "tab"], w=["prodk"])
            b2 = self.next_bank()
            P.op("pe", lambda e: e.matmul(self.bank[b2][:, 0:nt], lhsT=self.sel2B, rhs=prodk[:, 0:nt], start=True, stop=True), r=["prodk", "cB"], w=[self.bkey[b2]])
            P.op("act", lambda e: e.copy(out=self.kpe2[:, g0:g0 + nt], in_=self.bank[b2][:, 0:nt]), r=[self.bkey[b2]], w=[("kpe2", hf, gi)])
        self.proj_fm(self.w_in, [(KPE0, 128)], hT, DC, hTkeys, tgs, evac_pe, "wb", wbufs[0:1], preload=preload)
        self.tap("cqn%d" % hf, self.cqn, [128, 4, T], [("cq", j, g) for j in range(4) for g in range(2)])
        self.tap("ckvn%d" % hf, self.ckvn[:], [128, 4, S], [("ckv", j, g) for j in range(4) for g in range(2)])
        self.tap("kpe2_%d" % hf, self.kpe2[:], [128, S], [("kpe2", hf, g) for g in range(2)])

    def rwkv_proj(self, hf, hT, hTkeys, wbufs, tgs):
        P = self.P
        tsh = [self.cv(12 + 2.25 * i, [128, 520], F32) for i in range(2)]
        dscr = self.cv(17, [128, 512], F32)
        lscr = self.cv(19, [128, 512], F32)

        def evac(si, seg, gi, tg, bi):
            c0, n = seg
            t0, nt = tg
            tmp = tsh[gi % 2]
            tk = ("tsh", gi % 2)
            ck = ("carry", si)
            P.op("act", lambda e: e.copy(out=tmp[0:n, 1:1 + nt], in_=self.bank[bi][0:n, 0:nt]), r=[self.bkey[bi]], w=[tk])
            P.op("act", lambda e: e.copy(out=tmp[0:n, 0:1], in_=self.carry[0:n, si:si + 1]), r=[ck, tk], w=[tk])
            P.op("dve", lambda e: e.tensor_tensor(out=dscr[0:n, 0:nt], in0=tmp[0:n, 0:nt], in1=tmp[0:n, 1:1 + nt], op=ALU.subtract), r=[tk], w=["dscr"])
            if si < 24:
                dst = self.rkv[0:n, si, t0:t0 + nt]
                dk = ("rkv", si, gi)
            else:
                dst = lscr[0:n, 0:nt]
                dk = "lscr"
            P.op("dve", lambda e: e.scalar_tensor_tensor(out=dst, in0=dscr[0:n, 0:nt], scalar=self.pc[0:n, C_MU + si:C_MU + si + 1], in1=tmp[0:n, 1:1 + nt], op0=ALU.mult, op1=ALU.add),
                 r=["dscr", tk, "pc"], w=[dk])
            P.op("act", lambda e: e.copy(out=self.carry[0:n, si:si + 1], in_=tmp[0:n, nt:nt + 1]), r=[tk], w=[ck])
            if si == 24:
                P.op("act", lambda e: e.activation(out=self.lora[0:64, 0, t0:t0 + nt], in_=lscr[0:64, 0:nt], func=AF.Tanh), r=["lscr"], w=[("lora", 0, gi)])
                P.op("act", lambda e: e.copy(out=self.lora[64:128, 0, t0:t0 + nt], in_=lscr[64:128, 0:nt]), r=["lscr"], w=[("lora", 1, gi)])
            elif si > 24:
                P.op("act", lambda e: e.activation(out=self.lora[0:n, si - 24, t0:t0 + nt], in_=lscr[0:n, 0:nt], func=AF.Sigmoid), r=["lscr"], w=[("lora", si - 23, gi)])
        self.proj_fm(self.w_in, RW_SEGS, hT, DC, hTkeys, tgs, evac, "wb", wbufs)
        self.tap("rkv%d" % hf, self.rkv, [128, 24, T], [("rkv", s_, g) for s_ in range(24) for g in range(2)])
        self.tap("lora%d" % hf, self.lora, [128, 3, T], [("lora", s_, g) for s_ in range(4) for g in range(2)])

    def mla_attn(self, hf):
        P = self.P
        nkey = (hf + 1) * T
        nkt = nkey // 128
        scale = 192.0 ** -0.5
        base = 94
        QTn = self.cv(base, [128, T], BF16)
        prodq = self.cv(base + 2, [128, T], BF16)
        KT = self.cv(base + 4, [128, S], BF16)
        V = self.cv(base + 8, [128, 16, 128], BF16)
        Pm = [self.cv(base + 12 + 4 * i, [128, S], BF16) for i in range(2)]
        PT = [self.cv(base + 20 + 4 * i, [128, 16, 128], BF16) for i in range(2)]
        wq = [self.cv(base + 28 + 2 * i, [128, 4, 256], BF16) for i in range(2)]
        wkv = [self.cv(base + 32 + 2 * i, [128, 4, 256], BF16) for i in range(2)]
        stt = self.cv(base + 36, [128, 64], F32)
        cqk = [("cq", j, g) for j in range(4) for g in range(2)]
        ckk = [("ckv", j, g) for j in range(4) for g in range(2)]
        kpk = [("kpe2", h_, g) for h_ in range(hf + 1) for g in range(2)]
        P.op("dve", lambda e: e.memset(stt, 0.0), w=["stt"])
        PVB = 7
        for h in range(8):
            sl = h % 2
            wqk, wkk = ("wq", sl), ("wkv", sl)
            vq = lambda a, b: self.w_uq[:, a:b].rearrange("(kc p) c -> p kc c", p=128)
            q0 = h * 192
            self.load(wq[sl][:, :, 0:192], vq(q0, q0 + 192), [wqk], "wq%da" % sl, eng="pool")
            self.load(wq[sl][:, :, 192:224], vq(q0 + 160, q0 + 192), [wqk], "wq%db" % sl, eng="pool")
            self.load(wq[sl][:, :, 224:256], vq(q0 + 128, q0 + 160), [wqk], "wq%dc" % sl, eng="pool")
            P.op("dve", lambda e, sl=sl: e.tensor_scalar(out=wq[sl][:, :, 192:224], in0=wq[sl][:, :, 192:224], scalar1=-1.0, scalar2=None, op0=ALU.mult), r=[wqk], w=[wqk])
            self.load(wkv[sl][:], self.w_ukv[:, h * 256:(h + 1) * 256].rearrange("(kc p) c -> p kc c", p=128), [wkk], "wkv%d" % sl, eng="pool")
            for gi in range(T // 512):
                t0 = gi * 512
                bi = self.next_bank(4, 7)
                for kc in range(4):
                    P.op("pe", lambda e, bi=bi, kc=kc, sl=sl, t0=t0: e.matmul(self.bank[bi], lhsT=wq[sl][:, kc, 0:128], rhs=self.cqn[:, kc, t0:t0 + 512], start=(kc == 0), stop=(kc == 3)), r=[wqk] + cqk, w=[self.bkey[bi]])
                P.op("act", lambda e, bi=bi, t0=t0: e.copy(out=QTn[:, t0:t0 + 512], in_=self.bank[bi]), r=[self.bkey[bi]], w=["QTn"])
                bi = self.next_bank(4, 7)
                for kc in range(4):
                    P.op("pe", lambda e, bi=bi, kc=kc, sl=sl, t0=t0: e.matmul(self.bank[bi], lhsT=wq[sl][:, kc, 128:256], rhs=self.cqn[:, kc, t0:t0 + 512], start=(kc == 0), stop=(kc == 3)), r=[wqk] + cqk, w=[self.bkey[bi]])
                P.op("dve", lambda e, bi=bi, t0=t0: e.tensor_tensor(out=prodq[:, t0:t0 + 512], in0=self.bank[bi], in1=self.tab[:, hf * T + t0:hf * T + t0 + 512], op=ALU.mult), r=[self.bkey[bi], "tab"], w=["prodq"])
            for gi in range(nkey // 512):
                t0 = gi * 512
                bi = self.next_bank(4, 7)
                for kc in range(4):
                    P.op("pe", lambda e, bi=bi, kc=kc, sl=sl, t0=t0: e.matmul(self.bank[bi], lhsT=wkv[sl][:, kc, 0:128], rhs=self.ckvn[:, kc, t0:t0 + 512], start=(kc == 0), stop=(kc == 3)), r=[wkk] + ckk, w=[self.bkey[bi]])
                P.op("act", lambda e, bi=bi, t0=t0: e.copy(out=KT[:, t0:t0 + 512], in_=self.bank[bi]), r=[self.bkey[bi]], w=["KT"])
                bi = self.next_bank(4, 7)
                for j in range(4):
                    kt = gi * 4 + j
                    for kc in range(4):
                        P.op("pe", lambda e, bi=bi, kc=kc, sl=sl, kt=kt, j=j: e.matmul(self.bank[bi][:, j * 128:(j + 1) * 128], lhsT=self.ckvn[:, kc, kt * 128:(kt + 1) * 128], rhs=wkv[sl][:, kc, 128:256], start=(kc == 0), stop=(kc == 3)), r=[wkk] + ckk, w=[self.bkey[bi]])
                P.op("dve", lambda e, bi=bi, gi=gi: e.tensor_copy(out=V[:, gi * 4:(gi + 1) * 4, :], in_=self.bank[bi].rearrange("p (j t) -> p j t", j=4)), r=[self.bkey[bi]], w=["V"])
            def stageA(qt, h=h):
                gq = hf * NTT + qt
                nk = gq + 1
                nc_ = nk * 128
                ps = qt % 2
                pk, ptk = ("Pm", ps), ("PT", ps)
                c = (h * NTT + qt) % 16 * 4
                nkb = (nc_ + 511) // 512
                so = (qt % 2) * 1024 if nc_ <= 1024 else 0
                kb0 = so // 512
                for kb in range(nkb):
                    c0 = kb * 512
                    cw = min(512, nc_ - c0)
                    last = kb == nkb - 1
                    P.op("pe", lambda e, c0=c0, cw=cw, qt=qt, so=so: e.matmul(self.psS[:, so + c0:so + c0 + cw], lhsT=QTn[:, qt * 128:(qt + 1) * 128], rhs=KT[:, c0:c0 + cw], start=True, stop=False), r=["QTn", "KT"], w=[self.bkey[kb0 + kb]])
                    P.op("pe", lambda e, c0=c0, cw=cw, qt=qt, last=last, so=so: e.matmul(self.psS[:, so + c0:so + c0 + cw], lhsT=prodq[:, qt * 128:(qt + 1) * 128], rhs=self.kpe2[:, c0:c0 + cw], start=False, stop=(not last)), r=["prodq"] + kpk, w=[self.bkey[kb0 + kb]])
                    if last:
                        P.op("pe", lambda e, nc_=nc_, so=so: e.matmul(self.psS[:, so + nc_ - 128:so + nc_], lhsT=self.identB, rhs=self.cmaskB, start=False, stop=True), r=["cB"], w=[self.bkey[kb0 + kb]])
                sk = [self.bkey[kb0 + kb] for kb in range(nkb)]
                P.op("dve", lambda e, nc_=nc_, c=c, so=so: e.reduce_max(out=stt[:, c:c + 1], in_=self.psS[:, so:so + nc_], axis=AX.X), r=sk, w=[("stt", c)])
                P.op("dve", lambda e, c=c: e.tensor_scalar(out=stt[:, c + 1:c + 2], in0=stt[:, c:c + 1], scalar1=-scale, scalar2=None, op0=ALU.mult), r=[("stt", c)], w=[("stt1", c)])
                P.op("dve", lambda e, c=c: e.memset(stt[:, c + 2:c + 3], 0.0), w=[("stt2", c)])
                P.op("act", lambda e, nc_=nc_, c=c, ps=ps, so=so: e.activation(out=Pm[ps][:, 0:nc_], in_=self.psS[:, so:so + nc_], func=AF.Exp, bias=stt[:, c + 1:c + 2], scale=scale, accum_out=stt[:, c + 2:c + 3]),
                     r=sk + [("stt1", c), ("stt2", c)], w=[pk, ("stt2", c)])
                P.op("dve", lambda e, c=c: e.reciprocal(out=stt[:, c + 3:c + 4], in_=stt[:, c + 2:c + 3]), r=[("stt2", c)], w=[("stt3", c)])
                P.op("dve", lambda e, nc_=nc_, c=c, ps=ps: e.tensor_scalar(out=Pm[ps][:, 0:nc_], in0=Pm[ps][:, 0:nc_], scalar1=stt[:, c + 3:c + 4], scalar2=None, op0=ALU.mult), r=[pk, ("stt3", c)], w=[pk])
            def stageB(qt, h=h):
                gq = hf * NTT + qt
                nk = gq + 1
                ps = qt % 2
                pk, ptk = ("Pm", ps), ("PT", ps)
                for g0 in range(0, nk, 8):
                    gn = min(8, nk - g0)
                    bi = self.next_bank(4, 7)
                    bk = self.bank[bi].bitcast(BF16)
                    for j in range(gn):
                        kt = g0 + j
                        P.op("pe", lambda e, bk=bk, j=j, kt=kt, ps=ps: e.transpose(out=bk[:, j * 128:(j + 1) * 128], in_=Pm[ps][:, kt * 128:(kt + 1) * 128], identity=self.identB), r=[pk, "cB"], w=[self.bkey[bi]])
                    self.copy(self.evac_eng(), PT[ps][:, g0:g0 + gn, :], bk[:, 0:gn * 128].rearrange("p (j t) -> p j t", j=gn), r=[self.bkey[bi]], w=[ptk])
                for kt in range(nk):
                    P.op("pe", lambda e, kt=kt, qt=qt, ps=ps, nk=nk: e.matmul(self.bank[PVB][:, (qt % 4) * 128:(qt % 4 + 1) * 128], lhsT=V[:, kt, :], rhs=PT[ps][:, kt, :], start=(kt == 0), stop=(kt == nk - 1)), r=["V", ptk], w=[self.bkey[PVB]])
                if qt % 4 == 3:
                    q0_ = (qt - 3) * 128
                    P.op("act", lambda e, q0_=q0_, h=h: e.copy(out=self.y[:, 8 + h, q0_:q0_ + 512], in_=self.bank[PVB]), r=[self.bkey[PVB]], w=[("y", 8 + h)])
            stageA(0)
            for qt in range(1, NTT):
                stageA(qt)
                stageB(qt - 1)
            stageB(NTT - 1)
        self.tap("ymla%d" % hf, self.y[:, 8:16, :], [128, 8, T], [("y", 8 + h) for h in range(8)])

    def rwkv_chunks(self, hf):
        P = self.P
        B = 86
        w2a2 = self.cv(B, [128, 1024], BF16)
        g2a = self.cv(B + 2, [128, 1024], BF16)
        g2b = self.cv(B + 4, [32, 1024], BF16)
        self.load(w2a2, self.w2a2_d[:, :], ["w2a2"], "rw0", eng="pool")
        self.load(g2a, self.g2_d[0:128, :], ["g2a"], "rw1", eng="pool")
        self.load(g2b, self.g2_d[128:160, :], ["g2b"], "rw2", eng="pool")
        o = B + 6
        Bt = [self.cv(o + 4 * i, [128, 8, 128], F32) for i in range(6)]
        o += 24
        SQ = self.cv(o, [128, 8, 128], BF16)
        BON = self.cv(o + 2, [128, 8, 128], BF16)
        o += 4
        VT = self.cv(o, [128, 1024], F32)
        KtT = self.cv(o + 4, [128, 1024], F32)
        BtT = self.cv(o + 8, [128, 1024], F32)
        o += 12
        Nn, NnT, Wm, Tm, Makm, Nbqm, Nkqm = [self.cv(o + 2 * i, [128, 512], F32) for i in range(7)]
        o += 14
        ZTs = self.cv(o, [128, 256], F32)
        UTs = self.cv(o + 1, [128, 256], F32)
        o += 2
        Ytm = self.cv(o, [128, 1024], F32)
        G1 = self.cv(o + 4, [128, 8, 128], F32)
        gs = self.cv(o + 8, [128, 128], F32)
        o += 8.5
        assert o <= ARENA_KIB, o
        pc = self.pc
        bl = lambda c0: pc[:, c0:c0 + 8].unsqueeze(2).to_broadcast([128, 8, 128])
        bm = lambda ap, n: ap.unsqueeze(1).to_broadcast([128, n, 128])
        ps_lo = self.psS[:, 0:1024].rearrange("p (a b) -> p a b", a=8)
        ps_hi = self.psS[:, 1024:2048].rearrange("p (a b) -> p a b", a=8)
        kLO = [self.bkey[0], self.bkey[1]]
        kHI = [self.bkey[2], self.bkey[3]]
        v4 = lambda ap: ap.rearrange("p (a b) -> p a b", a=4)
        TT = lambda eng, out, in0, in1, op, r, w: P.op(eng, lambda e: e.tensor_tensor(out=out, in0=in0, in1=in1, op=op), r=r, w=w)
        STT = lambda out, in0, sc, in1, op0, op1, r, w: P.op("dve", lambda e: e.scalar_tensor_tensor(out=out, in0=in0, scalar=sc, in1=in1, op0=op0, op1=op1), r=r, w=w)
        ACT = lambda out, in_, func, r, w, **kw: P.op("act", lambda e: e.activation(out=out, in_=in_, func=func, **kw), r=r, w=w)
        MM = lambda out, lhsT, rhs, st_, sp_, r, w: P.op("pe", lambda e: e.matmul(out, lhsT=lhsT, rhs=rhs, start=st_, stop=sp_), r=r, w=w)
        b0, b1, b2, b3, b4, b5 = Bt
        for c in range(NTT):
            ct = slice(c * 128, (c + 1) * 128)
            r_ = self.rkv[:, 0:8, ct]
            k_ = self.rkv[:, 8:16, ct]
            v_ = self.rkv[:, 16:24, ct]
            for hp in range(8):
                fs = slice(hp * 128, (hp + 1) * 128)
                MM(self.psS[:, hp * 128:(hp + 1) * 128], w2a2[0:64, fs], self.lora[0:64, 0, ct], True, True, ["w2a2"], [kLO[hp // 4]])
                MM(self.psS[:, 1024 + hp * 128:1024 + (hp + 1) * 128], w2a2[64:128, fs], self.lora[64:128, 0, ct], True, True, ["w2a2"], [kHI[hp // 4]])
            TT("dve", b0, ps_lo, bl(C_W0), ALU.add, kLO + ["pc"], ["b0"])
            ACT(b0, b0, AF.Exp, ["b0"], ["b0"], scale=-1.0)
            ACT(b0, b0, AF.Ln, ["b0"], ["b0"], bias=1.0)
            ACT(b1, b0, AF.Exp, ["b0", "pc"], ["b1"], scale=-1.0, bias=self.col(C_MHALF))
            for hp in range(8):
                P.op("dve", lambda e, hp=hp: e.tensor_tensor_scan(out=b2[:, hp, :], data0=b1[:, hp, :], data1=self.onesF[:, 0:128], initial=0.0, op0=ALU.add, op1=ALU.mult),
                     r=["b1", "onesF"], w=["b2"])
            ACT(b3, b2, AF.Exp, ["b2"], ["b3"], scale=-1.0)
            ACT(b4, b2, AF.Exp, ["b2"], ["b4"])
            TT("pool", b0, b2, b1, ALU.subtract, ["b1", "b2"], ["b0"])
            ACT(b5, b0, AF.Exp, ["b0"], ["b5"], scale=-1.0)
            TT("dve", b0, ps_hi, bl(C_A0), ALU.add, kHI + ["pc"], ["b0"])
            ACT(b0, b0, AF.Exp, ["b0"], ["b0"], scale=-1.0)
            P.op("dve", lambda e: e.tensor_scalar(out=b0, in0=b0, scalar1=1.0, scalar2=None, op0=ALU.add), r=["b0"], w=["b0"])
            P.op("dve", lambda e: e.reciprocal(out=b0, in_=b0), r=["b0"], w=["b0"])
            TT("dve", b1, k_, bl(C_KK), ALU.mult, ["pc"], ["b1"])
            TT("pool", SQ, b1, b1, ALU.mult, ["b1"], ["SQ"])
            for hp in range(8):
                MM(self.psS[:, hp * 128:(hp + 1) * 128], self.onesbdB, SQ[:, hp, :], True, True, ["SQ", "cB"], [kLO[hp // 4]])
            P.op("dve", lambda e: e.tensor_scalar(out=b2, in0=ps_lo, scalar1=1e-24, scalar2=None, op0=ALU.max), r=kLO, w=["b2"])
            ACT(b2, b2, AF.Ln, ["b2"], ["b2"])
            ACT(b2, b2, AF.Exp, ["b2"], ["b2"], scale=-0.5)
            TT("dve", b1, b1, b2, ALU.mult, ["b1", "b2"], ["b1"])
            STT(b5, b1, -1.0, b5, ALU.mult, ALU.mult, ["b1", "b5"], ["b5"])
            TT("pool", b2, b1, b0, ALU.mult, ["b1", "b0"], ["b2"])
            TT("dve", b2, b2, b4, ALU.mult, ["b2", "b4"], ["b2"])
            STT(b0, b0, -1.0, bl(C_KA), ALU.add, ALU.mult, ["b0", "pc"], ["b0"])
            STT(b0, b0, 1.0, k_, ALU.add, ALU.mult, ["b0"], ["b0"])
            TT("dve", b1, r_, b0, ALU.mult, ["b0", "b1"], ["b1"])
            TT("dve", SQ, b1, bl(C_RK), ALU.mult, ["b1", "pc"], ["SQ"])
            for hp in range(8):
                MM(self.psS[:, 1024 + hp * 128:1024 + (hp + 1) * 128], self.onesbdB, SQ[:, hp, :], True, True, ["SQ", "cB"], [kHI[hp // 4]])
            TT("dve", BON, ps_hi, v_, ALU.mult, kHI, ["BON"])
            TT("pool", b0, b0, b4, ALU.mult, ["b0", "b4"], ["b0"])
            TT("dve", b1, r_, b3, ALU.mult, ["b3"], ["b1"])
            Kt_, Qt_, Bt_, Pd_, At_ = b0, b1, b2, b3, b5
            if c == 0:
                for nm, bb, kk_ in (("Kt", b0, "b0"), ("Qt", b1, "b1"), ("Bt", b2, "b2"), ("Pd", b3, "b3"), ("iP", b4, "b4"), ("At", b5, "b5")):
                    self.tap("%s%d" % (nm, hf), bb, [128, 8, 128], [kk_])
            bk4 = self.bank[4].bitcast(BF16)
            for hp in range(8):
                P.op("pe", lambda e, hp=hp, ct=ct: e.transpose(out=bk4[:, hp * 128:(hp + 1) * 128], in_=self.rkv[:, 16 + hp, ct], identity=self.identB), r=["cB"], w=[self.bkey[4]])
            P.op("act", lambda e: e.copy(out=VT, in_=bk4), r=[self.bkey[4]], w=["VT"])
            for hp in range(8):
                P.op("pe", lambda e, hp=hp: e.transpose(out=self.psS[:, hp * 128:(hp + 1) * 128], in_=Kt_[:, hp, :], identity=self.identF), r=["b0", "cF"], w=[kLO[hp // 4]])
                P.op("pe", lambda e, hp=hp: e.transpose(out=self.psS[:, 1024 + hp * 128:1024 + (hp + 1) * 128], in_=Bt_[:, hp, :], identity=self.identF), r=["b2", "cF"], w=[kHI[hp // 4]])
            P.op("act", lambda e: e.copy(out=KtT, in_=self.psS[:, 0:1024]), r=kLO, w=["KtT"])
            P.op("dve", lambda e: e.tensor_copy(out=BtT, in_=self.psS[:, 1024:2048]), r=kHI, w=["BtT"])
            for hg in range(4):
                hd = [(2 * hg + q // 2, (q % 2) * 64) for q in range(4)]
                qs = lambda q: slice(q * 128, (q + 1) * 128)
                for q, (pr, hs) in enumerate(hd):
                    sl = slice(hs, hs + 64)
                    MM(self.bank[0][:, qs(q)], Bt_[sl, pr, :], At_[sl, pr, :], True, True, ["b2", "b5"], [self.bkey[0]])
                    MM(self.bank[1][:, qs(q)], At_[sl, pr, :], Bt_[sl, pr, :], True, True, ["b2", "b5"], [self.bkey[1]])
                    MM(self.bank[2][:, qs(q)], Kt_[sl, pr, :], At_[sl, pr, :], True, True, ["b0", "b5"], [self.bkey[2]])
                    MM(self.bank[3][:, qs(q)], Bt_[sl, pr, :], Qt_[sl, pr, :], True, True, ["b2", "b1"], [self.bkey[3]])
                    MM(self.bank[4][:, qs(q)], Kt_[sl, pr, :], Qt_[sl, pr, :], True, True, ["b0", "b1"], [self.bkey[4]])
                TT("dve", v4(Nn), v4(self.bank[0]), bm(self.iu, 4), ALU.mult, [self.bkey[0], "cF"], ["Nn"])
                TT("dve", v4(NnT), v4(self.bank[1]), bm(self.il, 4), ALU.mult, [self.bkey[1], "cF"], ["NnT"])
                TT("dve", v4(Makm), v4(self.bank[2]), bm(self.iu, 4), ALU.mult, [self.bkey[2], "cF"], ["Makm"])
                TT("dve", v4(Nbqm), v4(self.bank[3]), bm(self.iue, 4), ALU.mult, [self.bkey[3], "cF"], ["Nbqm"])
                TT("dve", v4(Nkqm), v4(self.bank[4]), bm(self.iue, 4), ALU.mult, [self.bkey[4], "cF"], ["Nkqm"])
                TT("pool", v4(Tm), v4(Nn), bm(self.identF, 4), ALU.add, ["Nn", "cF"], ["Tm"])
                def sq_(lvl):
                    last = lvl == 5
                    for q in range(4):
                        if not last:
                            MM(self.bank[0][:, qs(q)], NnT[:, qs(q)], Nn[:, qs(q)], True, True, ["Nn", "NnT"], [self.bkey[0]])
                        MM(self.bank[1][:, qs(q)], Nn[:, qs(q)], NnT[:, qs(q)], True, True, ["Nn", "NnT"], [self.bkey[1]])

                def ev_(lvl):
                    last = lvl == 5
                    TT("dve", v4(Wm), v4(self.bank[1]), bm(self.identF, 4), ALU.add, [self.bkey[1], "cF"], ["Wm"])
                    if not last:
                        P.op("act", lambda e: e.copy(out=NnT, in_=self.bank[1]), r=[self.bkey[1]], w=["NnT"])
                        P.op("act", lambda e: e.copy(out=Nn, in_=self.bank[0]), r=[self.bkey[0]], w=["Nn"])

                def tm_(lvl):
                    for q in range(4):
                        MM(self.bank[2][:, qs(q)], Wm[:, qs(q)], Tm[:, qs(q)], True, True, ["Wm", "Tm"], [self.bkey[2]])
                    P.op("dve", lambda e: e.tensor_copy(out=Tm, in_=self.bank[2]), r=[self.bkey[2]], w=["Tm"])

                sq_(0)
                ev_(0)
                for lvl in range(1, 6):
                    sq_(lvl)
                    tm_(lvl - 1)
                    ev_(lvl)
                tm_(5)
                if c == 0 and hg == 0:
                    self.tap("Tm%d" % hf, Tm, [128, 512], ["Tm"])
                    self.tap("Makm%d" % hf, Makm, [128, 512], ["Makm"])
                    self.tap("Nbqm%d" % hf, Nbqm, [128, 512], ["Nbqm"])
                q64 = lambda q: slice(q * 64, (q + 1) * 64)
                for q, (pr, hs) in enumerate(hd):
                    sl = slice(hs, hs + 64)
                    MM(self.bank[5][:, q64(q)], At_[sl, pr, :], self.STp[sl, pr, hs:hs + 64], True, False, ["b5", ("STp", pr)], [self.bkey[5]])
                    MM(self.bank[5][:, q64(q)], Makm[:, qs(q)], VT[:, pr * 128 + hs:pr * 128 + hs + 64], False, True, ["Makm", "VT"], [self.bkey[5]])
                P.op("act", lambda e: e.copy(out=ZTs, in_=self.bank[5][:, 0:256]), r=[self.bkey[5]], w=["ZTs"])
                for q in range(4):
                    MM(self.bank[5][:, 256 + q * 64:256 + (q + 1) * 64], Tm[:, qs(q)], ZTs[:, q64(q)], True, True, ["Tm", "ZTs"], [self.bkey[5]])
                P.op("act", lambda e: e.copy(out=UTs, in_=self.bank[5][:, 256:512]), r=[self.bkey[5]], w=["UTs"])
                for q, (pr, hs) in enumerate(hd):
                    sl = slice(hs, hs + 64)
                    MM(self.bank[6][:, q64(q)], Qt_[sl, pr, :], self.STp[sl, pr, hs:hs + 64], True, False, ["b1", ("STp", pr)], [self.bkey[6]])
                    MM(self.bank[6][:, q64(q)], Nbqm[:, qs(q)], UTs[:, q64(q)], False, False, ["Nbqm", "UTs"], [self.bkey[6]])
                    MM(self.bank[6][:, q64(q)], Nkqm[:, qs(q)], VT[:, pr * 128 + hs:pr * 128 + hs + 64], False, True, ["Nkqm", "VT"], [self.bkey[6]])
                P.op("dve", lambda e, hg=hg: e.tensor_copy(out=Ytm[:, hg * 256:(hg + 1) * 256], in_=self.bank[6][:, 0:256]), r=[self.bkey[6]], w=[("Ytm", hg)])
                for pl in range(2):
                    pr = 2 * hg + pl
                    fs = slice(pr * 128, (pr + 1) * 128)
                    ob = self.bank[7][:, pl * 128:(pl + 1) * 128]
                    MM(ob, self.identF, self.STp[:, pr, :], True, False, ["cF", ("STp", pr)], [self.bkey[7]])
                    MM(ob, BtT[:, fs], UTs[:, pl * 128:(pl + 1) * 128], False, False, ["BtT", "UTs"], [self.bkey[7]])
                    MM(ob, KtT[:, fs], VT[:, fs], False, True, ["KtT", "VT"], [self.bkey[7]])
                for pl in range(2):
                    pr = 2 * hg + pl
                    for hs in (0, 64):
                        sl = slice(hs, hs + 64)
                        P.op("dve", lambda e, pr=pr, pl=pl, hs=hs, sl=sl: e.tensor_scalar(out=self.STp[sl, pr, hs:hs + 64], in0=self.bank[7][sl, pl * 128 + hs:pl * 128 + hs + 64],
                                                                                        scalar1=Pd_[sl, pr, 127:128], scalar2=None, op0=ALU.mult),
                             r=[self.bkey[7], "b3"], w=[("STp", pr)])
            Yk = [("Ytm", hg) for hg in range(4)]
            if c == 1:
                self.tap("Ytmc1_%d" % hf, Ytm, [128, 1024], Yk)
            if c == 0:
                self.tap("STp%d" % hf, self.STp[:], [128, 8, 128], [("STp", p_) for p_ in range(8)])
                self.tap("Ytm%d" % hf, Ytm, [128, 1024], Yk)
                self.tap("VT%d" % hf, VT, [128, 1024], ["VT"])
                self.tap("KtT%d" % hf, KtT, [128, 1024], ["KtT"])
            Y3 = Ytm.rearrange("p (h n) -> p h n", n=64)
            G1f = G1.rearrange("p a b -> p (a b)")
            P.op("dve", lambda e: e.tensor_reduce(out=gs[:, 0:16], in_=Y3, axis=AX.X, op=ALU.add), r=Yk, w=["gs0"])
            TT("pool", G1f, Ytm, Ytm, ALU.mult, Yk, ["G1"])
            P.op("dve", lambda e: e.tensor_reduce(out=gs[:, 16:32], in_=G1f.rearrange("p (h n) -> p h n", n=64), axis=AX.X, op=ALU.add), r=["G1"], w=["gs1"])
            P.op("dve", lambda e: e.tensor_scalar(out=gs[:, 32:48], in0=gs[:, 0:16], scalar1=1.0 / 64, scalar2=None, op0=ALU.mult), r=["gs0"], w=["gs2"])
            TT("dve", gs[:, 48:64], gs[:, 32:48], gs[:, 32:48], ALU.mult, ["gs2"], ["gs3"])
            STT(gs[:, 64:80], gs[:, 16:32], 1.0 / 64, gs[:, 48:64], ALU.mult, ALU.subtract, ["gs1", "gs3"], ["gs4"])
            ACT(gs[:, 80:96], gs[:, 64:80], AF.Ln, ["gs4", "pc"], ["gs5"], bias=self.col(C_GNEPS))
            ACT(gs[:, 96:112], gs[:, 80:96], AF.Exp, ["gs5"], ["gs6"], scale=-0.5)
            TT("dve", Y3, Y3, gs[:, 32:48].unsqueeze(2).to_broadcast([128, 16, 64]), ALU.subtract, Yk + ["gs2"], Yk)
            TT("dve", Y3, Y3, gs[:, 96:112].unsqueeze(2).to_broadcast([128, 16, 64]), ALU.mult, Yk + ["gs6"], Yk)
            for hp in range(8):
                fs = slice(hp * 128, (hp + 1) * 128)
                P.op("pe", lambda e, hp=hp, fs=fs: e.transpose(out=self.psS[:, fs], in_=Ytm[:, fs], identity=self.identF), r=Yk + ["cF"], w=[kLO[hp // 4]])
                MM(self.psS[:, 1024 + hp * 128:1024 + (hp + 1) * 128], g2a[:, fs], self.lora[:, 1, ct], True, False, ["g2a"], [kHI[hp // 4]])
                MM(self.psS[:, 1024 + hp * 128:1024 + (hp + 1) * 128], g2b[0:32, fs], self.lora[0:32, 2, ct], False, True, ["g2b"], [kHI[hp // 4]])
            TT("dve", G1, ps_lo, bl(C_GNW), ALU.mult, kLO + ["pc", "G1"], ["G1"])
            TT("dve", G1, G1, bl(C_GNB), ALU.add, ["G1", "pc"], ["G1"])
            TT("dve", G1, G1, BON, ALU.add, ["G1", "BON"], ["G1"])
            if c == 1:
                self.tap("gsc1_%d" % hf, gs, [128, 128], ["gs0", "gs1", "gs2", "gs3", "gs4", "gs5", "gs6"])
                self.tap("G1c1_%d" % hf, G1, [128, 8, 128], ["G1"])
                self.tap("BONc1_%d" % hf, BON, [128, 8, 128], ["BON"])
                self.tap("Ync1_%d" % hf, Ytm, [128, 1024], Yk)
            if c == 0:
                self.tap("gs%d" % hf, gs, [128, 128], ["gs0", "gs1", "gs2", "gs3", "gs4", "gs5", "gs6"])
                self.tap("G1_%d" % hf, G1, [128, 8, 128], ["G1"])
                self.tap("Yn%d" % hf, Ytm, [128, 1024], Yk)
            TT("dve", self.y[:, 0:8, ct], G1, ps_hi, ALU.mult, ["G1"] + kHI, [("y", hp) for hp in range(8)])
        self.tap("yrw%d" % hf, self.y[:, 0:8, :], [128, 8, T], [("y", hp) for hp in range(8)])

    def ffn_block(self, hf):
        P = self.P
        x1 = self.cv(32, [128, NTT, D], F32)
        wo = [self.cv(96 + 16 * i, [128, DC, 512], BF16) for i in range(2)]
        xin = [self.cv(128 + 2 * i, [128, 512], F32) for i in range(2)]
        ykeys = [("y", c) for c in range(16)]
        for ds in range(4):
            sl = ds % 2
            self.loadw(wo[sl], self.w_out, 0, ds * 512, 512, DC, [("wo", sl)], "wo%d" % sl, step=2)
            for tt in range(NTT):
                bi = self.next_bank()
                xs = (ds * NTT + tt) % 2
                self.load(xin[xs], self.x[hf * T + tt * 128:hf * T + (tt + 1) * 128, ds * 512:(ds + 1) * 512], [("xin", xs)], "xin%d" % xs)
                for kc in range(DC):
                    P.op("pe", lambda e, bi=bi, kc=kc, sl=sl, tt=tt: e.matmul(self.bank[bi], lhsT=self.y[:, kc, tt * 128:(tt + 1) * 128], rhs=wo[sl][:, kc, :], start=(kc == 0), stop=(kc == DC - 1)),
                         r=[("wo", sl)] + ykeys, w=[self.bkey[bi]])
                P.op("dve", lambda e, bi=bi, tt=tt, ds=ds, xs=xs: e.tensor_tensor(out=x1[:, tt, ds * 512:(ds + 1) * 512], in0=self.bank[bi], in1=xin[xs], op=ALU.add),
                     r=[self.bkey[bi], ("xin", xs)], w=[("x1", tt, ds)])
        P.barrier()
        P.mark("h%d w_out done" % hf)
        self.tap("x1_%d" % hf, x1, [128, NTT, D], [])
        if self.stage <= 5:
            return
        h2T = self.cv(0, [128, DC, T], BF16)
        self.norm_transpose(lambda tt: (x1[:, tt, :], []), 1, h2T, "F", 144)
        P.barrier()
        P.mark("h%d ffn norm done" % hf)
        wg = [self.cv(96 + 8 * i, [128, DC, 256], BF16) for i in range(2)]
        wu = [self.cv(112 + 8 * i, [128, DC, 256], BF16) for i in range(2)]
        wd = [self.cv(128 + 8 * i, [128, 2, D], BF16) for i in range(3)]
        actbs = [self.cv(152 + 4 * i, [128, 2, T], BF16) for i in range(2)]
        gbuf = self.cv(160, [128, 576], F32)
        cscr = self.cv(162.25, [128, 512], F32)
        NG = NFC // 2

        def down(g):
            sl3 = g % 3
            actb = actbs[g % 2]
            for tt in range(NTT):
                for ds in range(4):
                    bi = self.next_bank()
                    for j in range(2):
                        P.op("pe", lambda e, bi=bi, j=j, tt=tt, ds=ds, sl3=sl3, actb=actb: e.matmul(self.bank[bi], lhsT=actb[:, j, tt * 128:(tt + 1) * 128], rhs=wd[sl3][:, j, ds * 512:(ds + 1) * 512], start=(j == 0), stop=(j == 1)),
                             r=[("wd", sl3), ("actb", g % 2, j)], w=[self.bkey[bi]])
                    d = x1[:, tt, ds * 512:(ds + 1) * 512]
                    P.op("dve", lambda e, bi=bi, d=d: e.tensor_tensor(out=d, in0=self.bank[bi], in1=d, op=ALU.add), r=[self.bkey[bi]], w=[("x1", tt, ds)])

        for gi in range(NG):
            sl = gi % 2
            sl3 = gi % 3
            wk = ("wgu", sl)
            c0 = gi * 256
            actb = actbs[gi % 2]
            self.loadw(wg[sl], self.w_gate, 0, c0, 256, DC, [wk], "wg%d" % sl, step=4)
            self.loadw(wu[sl], self.w_up, 0, c0, 256, DC, [wk], "wu%d" % sl, step=4)
            for j in range(2):
                for q in range(2):
                    self.load(wd[sl3][:, j, q * 1024:(q + 1) * 1024], self.w_down[c0 + j * 128:c0 + (j + 1) * 128, q * 1024:(q + 1) * 1024], [("wd", sl3)], "wd%d" % sl3, eng="pool")
            for j in range(2):
                fc = gi * 2 + j
                cw = C_CW + fc * 3
                for tg in range(2):
                    t0 = tg * 512
                    bG = self.next_bank()
                    for kc in range(DC):
                        P.op("pe", lambda e, bG=bG, kc=kc, sl=sl, j=j, t0=t0: e.matmul(self.bank[bG], lhsT=wg[sl][:, kc, j * 128:(j + 1) * 128], rhs=h2T[:, kc, t0:t0 + 512], start=(kc == 0), stop=(kc == DC - 1)), r=[wk], w=[self.bkey[bG]])
                    bU = self.next_bank()
                    for kc in range(DC):
                        P.op("pe", lambda e, bU=bU, kc=kc, sl=sl, j=j, t0=t0: e.matmul(self.bank[bU], lhsT=wu[sl][:, kc, j * 128:(j + 1) * 128], rhs=h2T[:, kc, t0:t0 + 512], start=(kc == 0), stop=(kc == DC - 1)), r=[wk], w=[self.bkey[bU]])
                    P.op("act", lambda e, bG=bG: e.copy(out=gbuf[:, 2:514], in_=self.bank[bG]), r=[self.bkey[bG]], w=["gbuf"])
                    P.op("act", lambda e, fc=fc: e.copy(out=gbuf[:, 0:2], in_=self.gcarry[:, fc, :]), r=[("gc", fc), "gbuf"], w=["gbuf"])
                    P.op("dve", lambda e, cw=cw, fc=fc: e.tensor_scalar(out=cscr, in0=gbuf[:, 2:514], scalar1=self.col(cw + 2), scalar2=self.col(C_CB + fc), op0=ALU.mult, op1=ALU.add), r=["gbuf", "pc"], w=["cscr"])
                    P.op("dve", lambda e, cw=cw: e.scalar_tensor_tensor(out=cscr, in0=gbuf[:, 1:513], scalar=self.col(cw + 1), in1=cscr, op0=ALU.mult, op1=ALU.add), r=["gbuf", "cscr", "pc"], w=["cscr"])
                    P.op("dve", lambda e, cw=cw: e.scalar_tensor_tensor(out=cscr, in0=gbuf[:, 0:512], scalar=self.col(cw), in1=cscr, op0=ALU.mult, op1=ALU.add), r=["gbuf", "cscr", "pc"], w=["cscr"])
                    P.op("act", lambda e, fc=fc: e.copy(out=self.gcarry[:, fc, :], in_=gbuf[:, 512:514]), r=["gbuf"], w=[("gc", fc)])
                    P.op("act", lambda e: e.activation(out=cscr, in_=cscr, func=AF.Silu), r=["cscr"], w=["cscr"])
                    P.op("dve", lambda e, bU=bU, j=j, t0=t0, actb=actb: e.tensor_tensor(out=actb[:, j, t0:t0 + 512], in0=self.bank[bU], in1=cscr, op=ALU.mult), r=[self.bkey[bU], "cscr"], w=[("actb", gi % 2, j)])
            if gi >= 1:
                down(gi - 1)
        down(NG - 1)
        P.barrier()
        P.mark("h%d ffn main done" % hf)
        self.load(self.gA[:], self.gbd[2], ["gA"], "gA")
        ot = [self.cv(96 + 8 * i, [128, D], F32) for i in range(2)]
        junk = self.cv(112, [128, D], BF16)
        st_ = self.cv(116, [128, 4 * NTT], F32)
        P.op("dve", lambda e: e.memset(st_, 0.0), w=["fstat"])
        for tt in range(NTT):
            c = 4 * tt
            o_ = ot[tt % 2]
            ok = ("ot", tt % 2)
            P.op("act", lambda e, tt=tt, c=c: e.activation(out=junk, in_=x1[:, tt, :], func=AF.Square, scale=float(D) ** -0.5, accum_out=st_[:, c:c + 1]), r=["fstat"], w=["fjunk", ("fs", tt)])
            P.op("act", lambda e, c=c: e.activation(out=st_[:, c + 2:c + 3], in_=st_[:, c:c + 1], func=AF.Sqrt, bias=self.col(C_EPS), scale=1.0), r=[("fs", tt), "pc"], w=[("fs2", tt)])
            P.op("dve", lambda e, c=c: e.reciprocal(out=st_[:, c + 3:c + 4], in_=st_[:, c + 2:c + 3]), r=[("fs2", tt)], w=[("fs3", tt)])
            P.op("dve", lambda e, tt=tt, c=c, o_=o_: e.scalar_tensor_tensor(out=o_, in0=x1[:, tt, :], scalar=st_[:, c + 3:c + 4], in1=self.gA[:], op0=ALU.mult, op1=ALU.mult), r=[("fs3", tt), "gA"], w=[ok])
            r0 = hf * T + tt * 128
            for q in range(4):
                P.op("sp", lambda e, o_=o_, r0=r0, q=q: e.dma_start(out=self.out[r0:r0 + 128, q * 512:(q + 1) * 512], in_=o_[:, q * 512:(q + 1) * 512]), r=[ok], dma="out%d" % (tt % 2))


def _cols(v, n):
    v = np.asarray(v, np.float32).reshape(-1)
    pad = n * 128 - v.shape[0]
    if pad:
        v = np.concatenate([v, np.zeros(pad, np.float32)])
    return np.ascontiguousarray(v.reshape(n, 128).T)


def host_consts():
    ident = np.eye(128, dtype=np.float32)
    s = np.arange(128)
    iu = (s[:, None] < s[None, :]).astype(np.float32)
    iue = (s[:, None] <= s[None, :]).astype(np.float32)
    il = iu.T.copy()
    obd = np.zeros((128, 128), np.float32)
    obd[:64, :64] = 1.0
    obd[64:, 64:] = 1.0
    cmask = np.where(s[None, :] <= s[:, None], 0.0, -30000.0).astype(np.float32)
    sel2 = np.tile(np.eye(64, dtype=np.float32), (2, 2))
    return np.ascontiguousarray(np.concatenate([ident, iu, iue, il, obd, cmask, sel2], axis=1))


def make_in_maps(inp):
    f = lambda a: np.ascontiguousarray(np.asarray(a, np.float32))
    half = 32
    invf = (10000.0 ** (-(np.arange(half, dtype=np.float32)) / np.float32(half))).astype(np.float32)
    invf2 = np.concatenate([invf, invf, invf, invf])[:, None]
    offc = np.concatenate([np.full(64, math.pi / 2, np.float32), np.zeros(64, np.float32)])[:, None]
    convw = f(inp["ffn_conv_w"][0])
    convw_c = np.ascontiguousarray(convw.T.reshape(NFC, 128, 3).transpose(1, 0, 2).reshape(128, NFC * 3))
    pcols = np.concatenate([
        _cols(inp["rwkv_mu"][0], 27), _cols(inp["rwkv_w0"][0], 8), _cols(inp["rwkv_a0"][0], 8), _cols(inp["rwkv_k_k"][0], 8),
        _cols(inp["rwkv_k_a"][0], 8), _cols(inp["rwkv_r_k"][0], 8), _cols(inp["rwkv_gn_w"][0], 8), _cols(inp["rwkv_gn_b"][0], 8),
        _cols(inp["mla_q_norm_g"][0], 4), _cols(inp["mla_kv_norm_g"][0], 4), _cols(inp["ffn_conv_b"][0], NFC), convw_c,
        invf2.astype(np.float32), offc.astype(np.float32)], axis=1)
    assert pcols.shape == (128, 269), pcols.shape
    gains = np.stack([np.broadcast_to(f(inp["attn_norm_g"][0])[None, :], (128, D)),
                      np.broadcast_to(f(inp["ffn_norm_g"][0])[None, :], (128, D)),
                      np.broadcast_to(f(inp["final_norm_g"])[None, :], (128, D))])
    shared = {
        "c_f32": host_consts(), "pcols": np.ascontiguousarray(pcols), "gains_b": np.ascontiguousarray(gains),
        "w2a2": np.ascontiguousarray(np.concatenate([f(inp["rwkv_w2"][0]), f(inp["rwkv_a2"][0])], axis=0)),
        "g2": f(inp["rwkv_g2"][0]),
        "w_in": f(inp["w_in"][0]), "w_uq": f(inp["mla_w_uq"][0]), "w_ukv": f(inp["mla_w_ukv"][0]), "w_out": f(inp["w_out"][0]),
        "w_gate": f(inp["ffn_w_gate"][0]), "w_up": f(inp["ffn_w_up"][0]), "w_down": f(inp["ffn_w_down"][0]),
    }
    maps = []
    xs = f(inp["x"])
    pos = np.asarray(inp["positions"], np.int32)
    for b in range(xs.shape[0]):
        m = dict(shared)
        m["x"] = xs[b]
        m["pos_b"] = np.ascontiguousarray(np.broadcast_to(pos[b][None, :], (128, S)))
        maps.append(m)
    return maps


_NC_CACHE = {}


def kernel(**inputs):
    if "nc" not in _NC_CACHE:
        kb = K()
        _NC_CACHE["nc"] = kb.build()
        _NC_CACHE["names"] = set(kb.din.keys())
    nc = _NC_CACHE["nc"]
    maps = make_in_maps(inputs)
    maps = [{k: v for k, v in m.items() if k in _NC_CACHE["names"]} for m in maps]
    res = run_bass_kernel_spmd(nc, maps, core_ids=list(range(len(maps))))
    return np.stack([np.asarray(r["out"], np.float32) for r in res.results], axis=0)
```

```python
import math
from contextlib import ExitStack

import numpy as np
import concourse.bass as bass
import concourse.mybir as mybir
from concourse.bass_utils import run_bass_kernel_spmd

F32 = mybir.dt.float32
BF16 = mybir.dt.bfloat16
I32 = mybir.dt.int32
ALU = mybir.AluOpType
AF = mybir.ActivationFunctionType
AX = mybir.AxisListType

S = 2048
D = 2048
T = 1024
NH = S // T
NTT = T // 128
DC = D // 128
DIN = 4448
DFF = 5632
NFC = DFF // 128
HN = 64
NHEAD = 16
CH = 128
TWO_PI = 2.0 * math.pi

ENGS = ("pe", "act", "dve", "pool", "sp")


class Op:
    __slots__ = ("eng", "fn", "deps", "dma", "ms", "need")

    def __init__(self, eng, fn, dma):
        self.eng = eng
        self.fn = fn
        self.deps = []
        self.dma = dma
        self.ms = None
        self.need = False


class Prog:
    def __init__(self):
        self.q = {e: [] for e in ENGS}
        self.lastw = {}
        self.readers = {}
        self.last_dma = {}

    def op(self, eng, fn, r=(), w=(), dma=None):
        o = Op(eng, fn, dma)
        deps = set()
        isb = lambda k: isinstance(k, tuple) and k and k[0] == "bank"
        w = list(w) + [k for k in r if isb(k)]
        r = [k for k in r if not isb(k)]
        for k in r:
            lw = self.lastw.get(k)
            if lw is not None:
                deps.add(lw)
        for k in w:
            lw = self.lastw.get(k)
            if lw is not None:
                deps.add(lw)
            for rd in self.readers.get(k, ()):
                deps.add(rd)
        for d in deps:
            if d.eng == "pe" and eng == "pe" and d.dma is None and dma is None:
                continue
            o.deps.append(d)
            d.need = True
        for k in w:
            self.lastw[k] = o
            self.readers[k] = []
        for k in r:
            self.readers.setdefault(k, []).append(o)
        self.q[eng].append(o)
        if dma is not None:
            self.last_dma[dma] = o
        return o

    def mark(self, name):
        if not hasattr(self, "marks"):
            self.marks = []
        self.marks.append((name, {e: len(self.q[e]) for e in ENGS}))

    def barrier(self):
        lasts = [self.q[e][-1] for e in ENGS if self.q[e]]
        dmas = list(self.last_dma.values())
        self.last_dma = {}
        for e in ENGS:
            o = Op(e, None, None)
            for d in lasts + dmas:
                if d.fn is None:
                    continue
                if d.eng == "pe" and e == "pe" and d.dma is None:
                    continue
                o.deps.append(d)
                d.need = True
            self.q[e].append(o)
        self.lastw = {}
        self.readers = {}

    def emit(self, nc, stack):
        sems = {e: stack.enter_context(nc.semaphore("s_" + e)) for e in ENGS}
        dsems = {}
        cnt = {e: 0 for e in ENGS}
        dcnt = {}
        for e in ENGS:
            for o in self.q[e]:
                if o.dma is not None:
                    if o.dma not in dsems:
                        dsems[o.dma] = stack.enter_context(nc.semaphore("d_" + o.dma))
                        dcnt[o.dma] = 0
                    dcnt[o.dma] += 16
                    o.ms = dcnt[o.dma]
                elif o.need and o.fn is not None:
                    cnt[e] += 1
                    o.ms = cnt[e]
        self.nsem = len(sems) + len(dsems)
        self.counts = dict(cnt)
        block = stack.enter_context(nc.Block())
        prog = self

        def run(e, eng):
            waited = {}
            for o in prog.q[e]:
                for d in o.deps:
                    if d.dma is not None:
                        s, v = dsems[d.dma], d.ms
                    else:
                        s, v = sems[d.eng], d.ms
                    if waited.get(s.num, 0) >= v:
                        continue
                    eng.wait_ge(s, v)
                    waited[s.num] = v
                if o.fn is None:
                    continue
                ins = o.fn(eng)
                if o.dma is not None:
                    ins.then_inc(dsems[o.dma], 16)
                elif o.need:
                    ins.then_inc(sems[e], 1)

        @block.tensor
        def _(eng):
            run("pe", eng)

        @block.scalar
        def _(eng):
            run("act", eng)

        @block.vector
        def _(eng):
            run("dve", eng)

        @block.gpsimd
        def _(eng):
            run("pool", eng)

        @block.sync
        def _(eng):
            run("sp", eng)


RW_SEGS = [(j * 128, 128) for j in range(24)] + [(3072, 128), (3200, 128), (3328, 32)]
CQ0, CKV0, KPE0 = 3360, 3872, 4384
ARENA_KIB = 165
C_MU, C_W0, C_A0, C_KK, C_KA, C_RK, C_GNW, C_GNB, C_GQ, C_GKV, C_CB, C_CW = 0, 27, 35, 43, 51, 59, 67, 75, 83, 87, 91, 135
C_INVF, C_OFF, C_NW0, C_NA0, C_EPS, C_MHALF, C_ONE, C_GNEPS, C_EPS24 = 267, 268, 269, 277, 285, 286, 287, 288, 289
NPC_HOST = 269
NPC = 292


class K:
    def __init__(self, stage=99, taps=()):
        self.stage = stage
        self.taps = set(taps)
        self.nc = bass.Bass("TRN2", target_bir_lowering=False)
        self.P = Prog()
        self.din = {}
        self.rr = 0
        self.bank_rr = 0

    def dram_in(self, name, shape, dt=F32):
        t = self.nc.dram_tensor(name, list(shape), dt, kind="ExternalInput").ap()
        self.din[name] = t
        return t

    def sb(self, st, name, shape, dt):
        return st.enter_context(self.nc.sbuf_tensor(name, list(shape), dt))

    def cv(self, off_kib, shape, dt):
        esz = 2 if dt == BF16 else 4
        n = 1
        for d in shape[1:]:
            n *= d
        o = int(round(off_kib * 1024)) // 2
        assert o * 2 + n * esz <= ARENA_KIB * 1024, (off_kib, shape)
        ap = self.arena[0:shape[0], o:o + n * esz // 2]
        if dt != BF16:
            ap = ap.bitcast(dt)
        if len(shape) == 3:
            ap = ap.rearrange("p (a b) -> p a b", a=shape[1])
        return ap

    def tap(self, name, src_ap, shape, keys):
        if name not in self.taps:
            return
        o = self.nc.dram_tensor("tap_" + name, list(shape), src_ap.dtype, kind="ExternalOutput").ap()
        if len(shape) == 3:
            for a in range(shape[1]):
                self.P.op("sp", lambda e, a=a: e.dma_start(out=o[:, a, :], in_=src_ap[:, a, :]), r=keys, dma="tap_" + name)
        else:
            self.P.op("sp", lambda e: e.dma_start(out=o, in_=src_ap), r=keys, dma="tap_" + name)

    def load(self, dst_ap, src_ap, wkeys, grp, eng="sp"):
        return self.P.op(eng, lambda e: e.dma_start(out=dst_ap, in_=src_ap), w=wkeys, dma=grp)

    def loadw(self, dst3, W, r0, c0, ncols, KC, wkeys, grp, step=4):
        for k0 in range(0, KC, step):
            k1 = min(KC, k0 + step)
            src = W[r0 + k0 * 128:r0 + k1 * 128, c0:c0 + ncols].rearrange("(kc p) c -> p kc c", p=128)
            self.load(dst3[:, k0:k1, 0:ncols], src, wkeys, grp, eng="pool")

    def evac_eng(self):
        self.rr += 1
        return "act" if self.rr % 2 else "dve"

    def copy(self, eng, out, in_, r, w):
        if eng == "act":
            return self.P.op("act", lambda e: e.copy(out=out, in_=in_), r=r, w=w)
        return self.P.op(eng, lambda e: e.tensor_copy(out=out, in_=in_), r=r, w=w)

    def next_bank(self, lo=4, hi=8):
        b = lo + (self.bank_rr % (hi - lo))
        self.bank_rr += 1
        return b

    def col(self, c, p0=0, p1=128):
        return self.pc[p0:p1, c:c + 1]

    def build(self):
        nc, P = self.nc, self.P
        self.x = self.dram_in("x", [S, D])
        self.out = nc.dram_tensor("out", [S, D], F32, kind="ExternalOutput").ap()
        with ExitStack() as st:
            self.setup_consts(st)
            for hf in range(NH):
                if self.stage <= 0:
                    break
                self.half(hf)
                if self.stage < 99:
                    break
            P.barrier()
            P.emit(nc, st)
        return nc

    def setup_consts(self, st):
        nc, P = self.nc, self.P
        sb = lambda n, s, d: self.sb(st, n, s, d)
        self.psS = st.enter_context(nc.psum_tensor("psS", [128, 2048], F32))
        self.psX = [st.enter_context(nc.psum_tensor("psX%d" % i, [128, 512], F32)) for i in range(4)]
        self.bank = [self.psS[:, i * 512:(i + 1) * 512] for i in range(4)] + [p[:] for p in self.psX]
        self.bkey = [("bank", i) for i in range(8)]
        cf = self.dram_in("c_f32", [128, 7 * 128])
        self.cF = sb("cF", [128, 5 * 128], F32)
        self.cB = sb("cB", [128, 7 * 128], BF16)
        self.load(self.cF[:], cf[:, 0:640], ["cF"], "c0")
        self.load(self.cB[:], cf[:, :], ["cB"], "c1", eng="pool")
        self.identF, self.iu, self.iue, self.il, self.onesbdF = [self.cF[:, i * 128:(i + 1) * 128] for i in range(5)]
        self.identB = self.cB[:, 0:128]
        self.onesbdB = self.cB[:, 512:640]
        self.cmaskB = self.cB[:, 640:768]
        self.sel2B = self.cB[:, 768:896]
        self.onesB = sb("onesB", [128, 128], BF16)
        P.op("dve", lambda e: e.memset(self.onesB[:], 1.0), w=["onesB"])
        self.onesF = sb("onesF", [128, 128], F32)
        P.op("dve", lambda e: e.memset(self.onesF[:], 1.0), w=["onesF"])
        pc = self.dram_in("pcols", [128, NPC_HOST])
        self.pc = sb("pc", [128, NPC], F32)
        self.load(self.pc[:, 0:NPC_HOST], pc[:, :], ["pc"], "c2")
        P.op("dve", lambda e: e.tensor_scalar(out=self.pc[:, C_NW0:C_NW0 + 16], in0=self.pc[:, C_W0:C_W0 + 16], scalar1=-1.0, scalar2=None, op0=ALU.mult), r=["pc"], w=["pc"])
        for c, v in ((C_EPS, 1e-6), (C_MHALF, -0.5), (C_ONE, 1.0), (C_GNEPS, 64e-5), (C_EPS24, 1e-24)):
            P.op("dve", lambda e, c=c, v=v: e.memset(self.pc[:, c:c + 1], v), r=["pc"], w=["pc"])
        self.gbd = self.dram_in("gains_b", [3, 128, D])
        self.gA = sb("gA", [128, D], F32)
        self.carry = sb("carry", [128, 27], F32)
        P.op("dve", lambda e: e.memset(self.carry[:], 0.0), w=["carry"])
        self.gcarry = sb("gcarry", [128, NFC, 2], F32)
        P.op("dve", lambda e: e.memset(self.gcarry[:], 0.0), w=["gcarry"])
        self.STp = sb("STp", [128, 8, 128], F32)
        P.op("dve", lambda e: e.memset(self.STp[:], 0.0), w=["STp"])
        self.ckvn = sb("ckvn", [128, 4, S], BF16)
        self.kpe2 = sb("kpe2", [128, S], BF16)
        self.tab = sb("ropetab", [128, S], BF16)
        self.w2a2_d = self.dram_in("w2a2", [128, 1024])
        self.g2_d = self.dram_in("g2", [160, 1024])
        self.w_in = self.dram_in("w_in", [D, DIN])
        self.w_uq = self.dram_in("w_uq", [512, 1536])
        self.w_ukv = self.dram_in("w_ukv", [512, 2048])
        self.w_out = self.dram_in("w_out", [D, D])
        self.w_gate = self.dram_in("w_gate", [D, DFF])
        self.w_up = self.dram_in("w_up", [D, DFF])
        self.w_down = self.dram_in("w_down", [DFF, D])
        self.arena = sb("arena", [128, ARENA_KIB * 512], BF16)
        self.rope_tables()

    def rope_tables(self):
        P = self.P
        posb = self.dram_in("pos_b", [128, S], I32)
        pi_ = self.cv(0, [128, S], I32)
        ang = self.cv(8, [128, S], F32)
        t1 = self.cv(16, [128, S], F32)
        ki = self.cv(24, [128, S], I32)
        kf = self.cv(32, [128, S], F32)
        self.load(pi_, posb[:, :], ["pos_i"], "c7")
        P.op("dve", lambda e: e.tensor_copy(out=ang, in_=pi_), r=["pos_i"], w=["ang"])
        P.op("dve", lambda e: e.tensor_scalar(out=ang, in0=ang, scalar1=self.col(C_INVF), scalar2=None, op0=ALU.mult), r=["ang", "pc"], w=["ang"])
        C1 = 6.28125
        C2 = TWO_PI - C1
        P.op("dve", lambda e: e.tensor_scalar(out=t1, in0=ang, scalar1=self.col(C_OFF), scalar2=None, op0=ALU.add), r=["ang", "pc"], w=["rt1"])
        P.op("dve", lambda e: e.tensor_scalar(out=kf, in0=t1, scalar1=1.0 / TWO_PI, scalar2=None, op0=ALU.mult), r=["rt1"], w=["rkf"])
        P.op("dve", lambda e: e.tensor_copy(out=ki, in_=kf), r=["rkf"], w=["rki"])
        P.op("dve", lambda e: e.tensor_copy(out=kf, in_=ki), r=["rki"], w=["rkf"])
        P.op("dve", lambda e: e.scalar_tensor_tensor(out=t1, in0=kf, scalar=-C1, in1=t1, op0=ALU.mult, op1=ALU.add), r=["rkf", "rt1"], w=["rt1"])
        P.op("dve", lambda e: e.scalar_tensor_tensor(out=t1, in0=kf, scalar=-C2, in1=t1, op0=ALU.mult, op1=ALU.add), r=["rkf", "rt1"], w=["rt1"])
        P.op("dve", lambda e: e.tensor_scalar(out=kf, in0=t1, scalar1=math.pi, scalar2=-TWO_PI, op0=ALU.is_gt, op1=ALU.mult), r=["rt1"], w=["rkf"])
        P.op("dve", lambda e: e.tensor_tensor(out=t1, in0=t1, in1=kf, op=ALU.add), r=["rkf", "rt1"], w=["rt1"])
        P.op("dve", lambda e: e.tensor_scalar(out=kf, in0=t1, scalar1=-math.pi, scalar2=TWO_PI, op0=ALU.is_lt, op1=ALU.mult), r=["rt1"], w=["rkf"])
        P.op("dve", lambda e: e.tensor_tensor(out=t1, in0=t1, in1=kf, op=ALU.add), r=["rkf", "rt1"], w=["rt1"])
        P.op("dve", lambda e: e.tensor_scalar(out=t1, in0=t1, scalar1=-3.14159, scalar2=3.14159, op0=ALU.max, op1=ALU.min), r=["rt1"], w=["rt1"])
        P.op("act", lambda e: e.activation(out=self.tab[:], in_=t1, func=AF.Sin), r=["rt1"], w=["tab"])
        self.tap("tab", self.tab[:], [128, S], ["tab"])
        P.barrier()

    def norm_transpose(self, get_src, gidx, dstT, tag, off_kib):
        P = self.P
        hb = [self.cv(off_kib + 4 * i, [128, D], BF16) for i in range(2)]
        st_ = self.cv(off_kib + 8, [128, 4 * NTT], F32)
        self.load(self.gA[:], self.gbd[gidx], ["gA"], "gA")
        P.op("dve", lambda e: e.memset(st_, 0.0), w=[tag + "stat"])
        for tt in range(NTT):
            src, skeys = get_src(tt)
            c = 4 * tt
            h = hb[tt % 2]
            hk = (tag + "hb", tt % 2)
            P.op("act", lambda e, src=src, c=c, h=h: e.activation(out=h, in_=src, func=AF.Square, scale=float(D) ** -0.5, accum_out=st_[:, c:c + 1]),
                 r=skeys + [tag + "stat"], w=[hk, (tag + "st", tt)])
            P.op("act", lambda e, c=c: e.activation(out=st_[:, c + 2:c + 3], in_=st_[:, c:c + 1], func=AF.Sqrt, bias=self.col(C_EPS), scale=1.0), r=[(tag + "st", tt), "pc"], w=[(tag + "st2", tt)])
            P.op("dve", lambda e, c=c: e.reciprocal(out=st_[:, c + 3:c + 4], in_=st_[:, c + 2:c + 3]), r=[(tag + "st2", tt)], w=[(tag + "st3", tt)])
            P.op("dve", lambda e, src=src, c=c, h=h: e.scalar_tensor_tensor(out=h, in0=src, scalar=st_[:, c + 3:c + 4], in1=self.gA[:], op0=ALU.mult, op1=ALU.mult),
                 r=skeys + [(tag + "st3", tt), "gA"], w=[hk])
            import os
            if os.environ.get("K_DBG") == "1":
                continue
            for g in range(2):
                bi = self.next_bank()
                bk = self.bank[bi].bitcast(BF16)
                for j in range(8):
                    dc = g * 8 + j
                    P.op("pe", lambda e, bk=bk, j=j, dc=dc, h=h: e.transpose(out=bk[:, j * 128:(j + 1) * 128], in_=h[:, dc * 128:(dc + 1) * 128], identity=self.identB),
                         r=[hk, "cB"], w=[self.bkey[bi]])
                if os.environ.get("K_DBG") == "2":
                    continue
                self.copy(self.evac_eng(), dstT[:, g * 8:(g + 1) * 8, tt * 128:(tt + 1) * 128], bk.rearrange("p (j t) -> p j t", j=8),
                          r=[self.bkey[bi]], w=[(tag + "T", tt)])

    def proj_fm(self, W, segs, src, KC, srckeys, tgs, evac, wtag, wbufs, preload=None, wcols=512):
        P = self.P
        groups = []
        for si, (c0, n) in enumerate(segs):
            if groups and preload is None and groups[-1][1] == c0 and (c0 + n - groups[-1][0]) <= wcols:
                groups[-1][1] = c0 + n
                groups[-1][2].append((si, c0, n))
            else:
                groups.append([c0, c0 + n, [(si, c0, n)]])
        for gidx, (g0, g1, subs) in enumerate(groups):
            slot = gidx % len(wbufs)
            wt = wbufs[slot]
            wk = (wtag, slot)
            if preload is None:
                self.loadw(wt, W, 0, g0, g1 - g0, KC, [wk], "%s%d" % (wtag, slot), step=4)
            else:
                preload(subs[0][0], wt, wk)
            for (si, c0, n) in subs:
                off = c0 - g0
                for gi, (t0, nt) in enumerate(tgs):
                    bi = self.next_bank()
                    for kc in range(KC):
                        P.op("pe", lambda e, bi=bi, kc=kc, wt=wt, n=n, t0=t0, nt=nt, off=off: e.matmul(self.bank[bi][0:n, 0:nt], lhsT=wt[:, kc, off:off + n], rhs=src[:, kc, t0:t0 + nt], start=(kc == 0), stop=(kc == KC - 1)),
                             r=[wk] + srckeys, w=[self.bkey[bi]])
                    evac(si, (c0, n), gi, (t0, nt), bi)

    def half(self, hf):
        P = self.P
        x = self.x
        self.y = self.cv(0, [128, DC, T], BF16)
        self.rkv = self.cv(32, [128, 24, T], BF16)
        self.lora = self.cv(80, [128, 3, T], BF16)
        self.cqn = self.cv(86, [128, 4, T], BF16)
        hT = self.cv(94, [128, DC, T], BF16)
        wbufs = [self.cv(126 + 16 * i, [128, DC, 512], BF16) for i in range(2)]
        xt = [self.cv(126 + 8 * i, [128, D], F32) for i in range(2)]

        def get_src(tt):
            b = tt % 2
            self.load(xt[b], x[hf * T + tt * 128: hf * T + (tt + 1) * 128, :], [("xt", b)], "xt%d" % b)
            return xt[b], [("xt", b)]

        P.mark("h%d start" % hf)
        self.norm_transpose(get_src, 0, hT, "A", 142)
        P.mark("h%d normA done" % hf)
        hTkeys = [("AT", tt) for tt in range(NTT)]
        self.tap("hT%d" % hf, hT, [128, DC, T], hTkeys)
        if self.stage <= 1:
            return
        P.barrier()
        tgs = [(0, 512), (512, 512)]
        self.mla_proj(hf, hT, hTkeys, wbufs, tgs)
        P.mark("h%d mla_proj done" % hf)
        import os
        if os.environ.get("K_DBG") not in ("3", "4"):
            self.rwkv_proj(hf, hT, hTkeys, wbufs, tgs)
        P.barrier()
        P.mark("h%d rwkv_proj done" % hf)
        if self.stage <= 2:
            return
        self.mla_attn(hf)
        P.barrier()
        P.mark("h%d mla_attn done" % hf)
        if self.stage <= 3:
            return
        self.rwkv_chunks(hf)
        P.barrier()
        P.mark("h%d rwkv_chunks done" % hf)
        if self.stage <= 4:
            return
        self.ffn_block(hf)
        P.barrier()
        P.mark("h%d ffn_block done" % hf)

    def mla_proj(self, hf, hT, hTkeys, wbufs, tgs):
        import os
        P = self.P
        sq = self.cv(0, [128, 4, T], BF16)
        rstd_b = self.cv(8, [128, 512], F32)
        prodk = self.cv(10, [128, 512], BF16)
        for which, c0, dst, toff, gcol in (("q", CQ0, self.cqn, 0, C_GQ), ("kv", CKV0, self.ckvn, hf * T, C_GKV)):
            def evac(si, seg, gi, tg, bi, dst=dst, toff=toff, which=which):
                t0, nt = tg
                P.op("act", lambda e: e.activation(out=sq[:, si, t0:t0 + nt], in_=self.bank[bi][:, 0:nt], func=AF.Square), r=[self.bkey[bi]], w=[("sq", si, gi)])
                P.op("dve", lambda e: e.tensor_copy(out=dst[:, si, toff + t0:toff + t0 + nt], in_=self.bank[bi][:, 0:nt]), r=[self.bkey[bi]], w=[("c" + which, si, gi)])
            self.proj_fm(self.w_in, [(c0 + j * 128, 128) for j in range(4)], hT, DC, hTkeys, tgs, evac, "wb", wbufs)
            for gi, (t0, nt) in enumerate(tgs):
                if os.environ.get("K_DBG2") == "1":
                    continue
                bi = self.next_bank()
                for j in range(4):
                    P.op("pe", lambda e, j=j, bi=bi, t0=t0, nt=nt: e.matmul(self.bank[bi][:, 0:nt], lhsT=self.onesB[:], rhs=sq[:, j, t0:t0 + nt], start=(j == 0), stop=(j == 3)),
                         r=[("sq", j, gi), "onesB"], w=[self.bkey[bi]])
                P.op("act", lambda e, bi=bi: e.activation(out=rstd_b[:, 0:nt], in_=self.bank[bi][:, 0:nt], func=AF.Sqrt, bias=self.col(C_EPS), scale=1.0 / 512), r=[self.bkey[bi], "pc"], w=["rstd_b"])
                P.op("dve", lambda e: e.reciprocal(out=rstd_b[:, 0:nt], in_=rstd_b[:, 0:nt]), r=["rstd_b"], w=["rstd_b"])
                for j in range(4):
                    d = dst[:, j, toff + t0:toff + t0 + nt]
                    P.op("dve", lambda e, d=d, j=j, gcol=gcol: e.scalar_tensor_tensor(out=d, in0=d, scalar=self.col(gcol + j), in1=rstd_b[:, 0:nt], op0=ALU.mult, op1=ALU.mult),
                         r=[("c" + which, j, gi), "rstd_b", "pc"], w=[("c" + which, j, gi)])
        import os
        if os.environ.get("K_DBG") == "3":
            return
        def preload(si, wt, wk):
            W = self.w_in
            v = lambda a, b: W[:, a:b].rearrange("(kc p) c -> p kc c", p=128)
            self.load(wt[:, :, 0:64], v(KPE0, KPE0 + 64), [wk], "wpe0", eng="pool")
            self.load(wt[:, :, 64:96], v(KPE0 + 32, KPE0 + 64), [wk], "wpe1", eng="pool")
            self.load(wt[:, :, 96:128], v(KPE0, KPE0 + 32), [wk], "wpe2", eng="pool")
            P.op("dve", lambda e: e.tensor_scalar(out=wt[:, :, 64:96], in0=wt[:, :, 64:96], scalar1=-1.0, scalar2=None, op0=ALU.mult), r=[wk], w=[wk])

        def evac_pe(si, seg, gi, tg, bi):
            t0, nt = tg
            g0 = hf * T + t0
            P.op("dve", lambda e: e.tensor_tensor(out=prodk[:, 0:nt], in0=self.bank[bi][:, 0:nt], in1=self.tab[:, g0:g0 + nt], op=ALU.mult), r=[self.bkey[bi], "tab"], w=["prodk"])
            b2 = self.next_bank()
            P.op("pe", lambda e: e.matmul(self.bank[b2][:, 0:nt], lhsT=self.sel2B, rhs=prodk[:, 0:nt], start=True, stop=True), r=["prodk", "cB"], w=[self.bkey[b2]])
            P.op("act", lambda e: e.copy(out=self.kpe2[:, g0:g0 + nt], in_=self.bank[b2][:, 0:nt]), r=[self.bkey[b2]], w=[("kpe2", hf, gi)])
        self.proj_fm(self.w_in, [(KPE0, 128)], hT, DC, hTkeys, tgs, evac_pe, "wb", wbufs[0:1], preload=preload)
        self.tap("cqn%d" % hf, self.cqn, [128, 4, T], [("cq", j, g) for j in range(4) for g in range(2)])
        self.tap("ckvn%d" % hf, self.ckvn[:], [128, 4, S], [("ckv", j, g) for j in range(4) for g in range(2)])
        self.tap("kpe2_%d" % hf, self.kpe2[:], [128, S], [("kpe2", hf, g) for g in range(2)])

    def rwkv_proj(self, hf, hT, hTkeys, wbufs, tgs):
        P = self.P
        tsh = [self.cv(12 + 2.25 * i, [128, 520], F32) for i in range(2)]
        dscr = self.cv(17, [128, 512], F32)
        lscr = self.cv(19, [128, 512], F32)

        def evac(si, seg, gi, tg, bi):
            c0, n = seg
            t0, nt = tg
            tmp = tsh[gi % 2]
            tk = ("tsh", gi % 2)
            ck = ("carry", si)
            P.op("act", lambda e: e.copy(out=tmp[0:n, 1:1 + nt], in_=self.bank[bi][0:n, 0:nt]), r=[self.bkey[bi]], w=[tk])
            P.op("act", lambda e: e.copy(out=tmp[0:n, 0:1], in_=self.carry[0:n, si:si + 1]), r=[ck, tk], w=[tk])
            P.op("dve", lambda e: e.tensor_tensor(out=dscr[0:n, 0:nt], in0=tmp[0:n, 0:nt], in1=tmp[0:n, 1:1 + nt], op=ALU.subtract), r=[tk], w=["dscr"])
            if si < 24:
                dst = self.rkv[0:n, si, t0:t0 + nt]
                dk = ("rkv", si, gi)
            else:
                dst = lscr[0:n, 0:nt]
                dk = "lscr"
            P.op("dve", lambda e: e.scalar_tensor_tensor(out=dst, in0=dscr[0:n, 0:nt], scalar=self.pc[0:n, C_MU + si:C_MU + si + 1], in1=tmp[0:n, 1:1 + nt], op0=ALU.mult, op1=ALU.add),
                 r=["dscr", tk, "pc"], w=[dk])
            P.op("act", lambda e: e.copy(out=self.carry[0:n, si:si + 1], in_=tmp[0:n, nt:nt + 1]), r=[tk], w=[ck])
            if si == 24:
                P.op("act", lambda e: e.activation(out=self.lora[0:64, 0, t0:t0 + nt], in_=lscr[0:64, 0:nt], func=AF.Tanh), r=["lscr"], w=[("lora", 0, gi)])
                P.op("act", lambda e: e.copy(out=self.lora[64:128, 0, t0:t0 + nt], in_=lscr[64:128, 0:nt]), r=["lscr"], w=[("lora", 1, gi)])
            elif si > 24:
                P.op("act", lambda e: e.activation(out=self.lora[0:n, si - 24, t0:t0 + nt], in_=lscr[0:n, 0:nt], func=AF.Sigmoid), r=["lscr"], w=[("lora", si - 23, gi)])
        self.proj_fm(self.w_in, RW_SEGS, hT, DC, hTkeys, tgs, evac, "wb", wbufs)
        self.tap("rkv%d" % hf, self.rkv, [128, 24, T], [("rkv", s_, g) for s_ in range(24) for g in range(2)])
        self.tap("lora%d" % hf, self.lora, [128, 3, T], [("lora", s_, g) for s_ in range(4) for g in range(2)])

    def mla_attn(self, hf):
        P = self.P
        nkey = (hf + 1) * T
        nkt = nkey // 128
        scale = 192.0 ** -0.5
        base = 94
        QTn = self.cv(base, [128, T], BF16)
        prodq = self.cv(base + 2, [128, T], BF16)
        KT = self.cv(base + 4, [128, S], BF16)
        V = self.cv(base + 8, [128, 16, 128], BF16)
        Pm = [self.cv(base + 12 + 4 * i, [128, S], BF16) for i in range(2)]
        PT = [self.cv(base + 20 + 4 * i, [128, 16, 128], BF16) for i in range(2)]
        wq = [self.cv(base + 28 + 2 * i, [128, 4, 256], BF16) for i in range(2)]
        wkv = [self.cv(base + 32 + 2 * i, [128, 4, 256], BF16) for i in range(2)]
        stt = self.cv(base + 36, [128, 64], F32)
        cqk = [("cq", j, g) for j in range(4) for g in range(2)]
        ckk = [("ckv", j, g) for j in range(4) for g in range(2)]
        kpk = [("kpe2", h_, g) for h_ in range(hf + 1) for g in range(2)]
        P.op("dve", lambda e: e.memset(stt, 0.0), w=["stt"])
        PVB = 7
        for h in range(8):
            sl = h % 2
            wqk, wkk = ("wq", sl), ("wkv", sl)
            vq = lambda a, b: self.w_uq[:, a:b].rearrange("(kc p) c -> p kc c", p=128)
            q0 = h * 192
            self.load(wq[sl][:, :, 0:192], vq(q0, q0 + 192), [wqk], "wq%da" % sl, eng="pool")
            self.load(wq[sl][:, :, 192:224], vq(q0 + 160, q0 + 192), [wqk], "wq%db" % sl, eng="pool")
            self.load(wq[sl][:, :, 224:256], vq(q0 + 128, q0 + 160), [wqk], "wq%dc" % sl, eng="pool")
            P.op("dve", lambda e, sl=sl: e.tensor_scalar(out=wq[sl][:, :, 192:224], in0=wq[sl][:, :, 192:224], scalar1=-1.0, scalar2=None, op0=ALU.mult), r=[wqk], w=[wqk])
            self.load(wkv[sl][:], self.w_ukv[:, h * 256:(h + 1) * 256].rearrange("(kc p) c -> p kc c", p=128), [wkk], "wkv%d" % sl, eng="pool")
            for gi in range(T // 512):
                t0 = gi * 512
                bi = self.next_bank(4, 7)
                for kc in range(4):
                    P.op("pe", lambda e, bi=bi, kc=kc, sl=sl, t0=t0: e.matmul(self.bank[bi], lhsT=wq[sl][:, kc, 0:128], rhs=self.cqn[:, kc, t0:t0 + 512], start=(kc == 0), stop=(kc == 3)), r=[wqk] + cqk, w=[self.bkey[bi]])
                P.op("act", lambda e, bi=bi, t0=t0: e.copy(out=QTn[:, t0:t0 + 512], in_=self.bank[bi]), r=[self.bkey[bi]], w=["QTn"])
                bi = self.next_bank(4, 7)
                for kc in range(4):
                    P.op("pe", lambda e, bi=bi, kc=kc, sl=sl, t0=t0: e.matmul(self.bank[bi], lhsT=wq[sl][:, kc, 128:256], rhs=self.cqn[:, kc, t0:t0 + 512], start=(kc == 0), stop=(kc == 3)), r=[wqk] + cqk, w=[self.bkey[bi]])
                P.op("dve", lambda e, bi=bi, t0=t0: e.tensor_tensor(out=prodq[:, t0:t0 + 512], in0=self.bank[bi], in1=self.tab[:, hf * T + t0:hf * T + t0 + 512], op=ALU.mult), r=[self.bkey[bi], "tab"], w=["prodq"])
            for gi in range(nkey // 512):
                t0 = gi * 512
                bi = self.next_bank(4, 7)
                for kc in range(4):
                    P.op("pe", lambda e, bi=bi, kc=kc, sl=sl, t0=t0: e.matmul(self.bank[bi], lhsT=wkv[sl][:, kc, 0:128], rhs=self.ckvn[:, kc, t0:t0 + 512], start=(kc == 0), stop=(kc == 3)), r=[wkk] + ckk, w=[self.bkey[bi]])
                P.op("act", lambda e, bi=bi, t0=t0: e.copy(out=KT[:, t0:t0 + 512], in_=self.bank[bi]), r=[self.bkey[bi]], w=["KT"])
                bi = self.next_bank(4, 7)
                for j in range(4):
                    kt = gi * 4 + j
                    for kc in range(4):
                        P.op("pe", lambda e, bi=bi, kc=kc, sl=sl, kt=kt, j=j: e.matmul(self.bank[bi][:, j * 128:(j + 1) * 128], lhsT=self.ckvn[:, kc, kt * 128:(kt + 1) * 128], rhs=wkv[sl][:, kc, 128:256], start=(kc == 0), stop=(kc == 3)), r=[wkk] + ckk, w=[self.bkey[bi]])
                P.op("dve", lambda e, bi=bi, gi=gi: e.tensor_copy(out=V[:, gi * 4:(gi + 1) * 4, :], in_=self.bank[bi].rearrange("p (j t) -> p j t", j=4)), r=[self.bkey[bi]], w=["V"])
            def stageA(qt, h=h):
                gq = hf * NTT + qt
                nk = gq + 1
                nc_ = nk * 128
                ps = qt % 2
                pk, ptk = ("Pm", ps), ("PT", ps)
                c = (h * NTT + qt) % 16 * 4
                nkb = (nc_ + 511) // 512
                so = (qt % 2) * 1024 if nc_ <= 1024 else 0
                kb0 = so // 512
                for kb in range(nkb):
                    c0 = kb * 512
                    cw = min(512, nc_ - c0)
                    last = kb == nkb - 1
                    P.op("pe", lambda e, c0=c0, cw=cw, qt=qt, so=so: e.matmul(self.psS[:, so + c0:so + c0 + cw], lhsT=QTn[:, qt * 128:(qt + 1) * 128], rhs=KT[:, c0:c0 + cw], start=True, stop=False), r=["QTn", "KT"], w=[self.bkey[kb0 + kb]])
                    P.op("pe", lambda e, c0=c0, cw=cw, qt=qt, last=last, so=so: e.matmul(self.psS[:, so + c0:so + c0 + cw], lhsT=prodq[:, qt * 128:(qt + 1) * 128], rhs=self.kpe2[:, c0:c0 + cw], start=False, stop=(not last)), r=["prodq"] + kpk, w=[self.bkey[kb0 + kb]])
                    if last:
                        P.op("pe", lambda e, nc_=nc_, so=so: e.matmul(self.psS[:, so + nc_ - 128:so + nc_], lhsT=self.identB, rhs=self.cmaskB, start=False, stop=True), r=["cB"], w=[self.bkey[kb0 + kb]])
                sk = [self.bkey[kb0 + kb] for kb in range(nkb)]
                P.op("dve", lambda e, nc_=nc_, c=c, so=so: e.reduce_max(out=stt[:, c:c + 1], in_=self.psS[:, so:so + nc_], axis=AX.X), r=sk, w=[("stt", c)])
                P.op("dve", lambda e, c=c: e.tensor_scalar(out=stt[:, c + 1:c + 2], in0=stt[:, c:c + 1], scalar1=-scale, scalar2=None, op0=ALU.mult), r=[("stt", c)], w=[("stt1", c)])
                P.op("dve", lambda e, c=c: e.memset(stt[:, c + 2:c + 3], 0.0), w=[("stt2", c)])
                P.op("act", lambda e, nc_=nc_, c=c, ps=ps, so=so: e.activation(out=Pm[ps][:, 0:nc_], in_=self.psS[:, so:so + nc_], func=AF.Exp, bias=stt[:, c + 1:c + 2], scale=scale, accum_out=stt[:, c + 2:c + 3]),
                     r=sk + [("stt1", c), ("stt2", c)], w=[pk, ("stt2", c)])
                P.op("dve", lambda e, c=c: e.reciprocal(out=stt[:, c + 3:c + 4], in_=stt[:, c + 2:c + 3]), r=[("stt2", c)], w=[("stt3", c)])
                P.op("dve", lambda e, nc_=nc_, c=c, ps=ps: e.tensor_scalar(out=Pm[ps][:, 0:nc_], in0=Pm[ps][:, 0:nc_], scalar1=stt[:, c + 3:c + 4], scalar2=None, op0=ALU.mult), r=[pk, ("stt3", c)], w=[pk])
            def stageB(qt, h=h):
                gq = hf * NTT + qt
                nk = gq + 1
                ps = qt % 2
                pk, ptk = ("Pm", ps), ("PT", ps)
                for g0 in range(0, nk, 8):
                    gn = min(8, nk - g0)
                    bi = self.next_bank(4, 7)
                    bk = self.bank[bi].bitcast(BF16)
                    for j in range(gn):
                        kt = g0 + j
                        P.op("pe", lambda e, bk=bk, j=j, kt=kt, ps=ps: e.transpose(out=bk[:, j * 128:(j + 1) * 128], in_=Pm[ps][:, kt * 128:(kt + 1) * 128], identity=self.identB), r=[pk, "cB"], w=[self.bkey[bi]])
                    self.copy(self.evac_eng(), PT[ps][:, g0:g0 + gn, :], bk[:, 0:gn * 128].rearrange("p (j t) -> p j t", j=gn), r=[self.bkey[bi]], w=[ptk])
                for kt in range(nk):
                    P.op("pe", lambda e, kt=kt, qt=qt, ps=ps, nk=nk: e.matmul(self.bank[PVB][:, (qt % 4) * 128:(qt % 4 + 1) * 128], lhsT=V[:, kt, :], rhs=PT[ps][:, kt, :], start=(kt == 0), stop=(kt == nk - 1)), r=["V", ptk], w=[self.bkey[PVB]])
                if qt % 4 == 3:
                    q0_ = (qt - 3) * 128
                    P.op("act", lambda e, q0_=q0_, h=h: e.copy(out=self.y[:, 8 + h, q0_:q0_ + 512], in_=self.bank[PVB]), r=[self.bkey[PVB]], w=[("y", 8 + h)])
            stageA(0)
            for qt in range(1, NTT):
                stageA(qt)
                stageB(qt - 1)
            stageB(NTT - 1)
        self.tap("ymla%d" % hf, self.y[:, 8:16, :], [128, 8, T], [("y", 8 + h) for h in range(8)])

    def rwkv_chunks(self, hf):
        P = self.P
        B = 86
        w2a2 = self.cv(B, [128, 1024], BF16)
        g2a = self.cv(B + 2, [128, 1024], BF16)
        g2b = self.cv(B + 4, [32, 1024], BF16)
        self.load(w2a2, self.w2a2_d[:, :], ["w2a2"], "rw0", eng="pool")
        self.load(g2a, self.g2_d[0:128, :], ["g2a"], "rw1", eng="pool")
        self.load(g2b, self.g2_d[128:160, :], ["g2b"], "rw2", eng="pool")
        o = B + 6
        Bt = [self.cv(o + 4 * i, [128, 8, 128], F32) for i in range(6)]
        o += 24
        SQ = self.cv(o, [128, 8, 128], BF16)
        BON = self.cv(o + 2, [128, 8, 128], BF16)
        o += 4
        VT = self.cv(o, [128, 1024], F32)
        KtT = self.cv(o + 4, [128, 1024], F32)
        BtT = self.cv(o + 8, [128, 1024], F32)
        o += 12
        Nn, NnT, Wm, Tm, Makm, Nbqm, Nkqm = [self.cv(o + 2 * i, [128, 512], F32) for i in range(7)]
        o += 14
        ZTs = self.cv(o, [128, 256], F32)
        UTs = self.cv(o + 1, [128, 256], F32)
        o += 2
        Ytm = self.cv(o, [128, 1024], F32)
        G1 = self.cv(o + 4, [128, 8, 128], F32)
        gs = self.cv(o + 8, [128, 128], F32)
        o += 8.5
        assert o <= ARENA_KIB, o
        pc = self.pc
        bl = lambda c0: pc[:, c0:c0 + 8].unsqueeze(2).to_broadcast([128, 8, 128])
        bm = lambda ap, n: ap.unsqueeze(1).to_broadcast([128, n, 128])
        ps_lo = self.psS[:, 0:1024].rearrange("p (a b) -> p a b", a=8)
        ps_hi = self.psS[:, 1024:2048].rearrange("p (a b) -> p a b", a=8)
        kLO = [self.bkey[0], self.bkey[1]]
        kHI = [self.bkey[2], self.bkey[3]]
        v4 = lambda ap: ap.rearrange("p (a b) -> p a b", a=4)
        TT = lambda eng, out, in0, in1, op, r, w: P.op(eng, lambda e: e.tensor_tensor(out=out, in0=in0, in1=in1, op=op), r=r, w=w)
        STT = lambda out, in0, sc, in1, op0, op1, r, w: P.op("dve", lambda e: e.scalar_tensor_tensor(out=out, in0=in0, scalar=sc, in1=in1, op0=op0, op1=op1), r=r, w=w)
        ACT = lambda out, in_, func, r, w, **kw: P.op("act", lambda e: e.activation(out=out, in_=in_, func=func, **kw), r=r, w=w)
        MM = lambda out, lhsT, rhs, st_, sp_, r, w: P.op("pe", lambda e: e.matmul(out, lhsT=lhsT, rhs=rhs, start=st_, stop=sp_), r=r, w=w)
        b0, b1, b2, b3, b4, b5 = Bt
        for c in range(NTT):
            ct = slice(c * 128, (c + 1) * 128)
            r_ = self.rkv[:, 0:8, ct]
            k_ = self.rkv[:, 8:16, ct]
            v_ = self.rkv[:, 16:24, ct]
            for hp in range(8):
                fs = slice(hp * 128, (hp + 1) * 128)
                MM(self.psS[:, hp * 128:(hp + 1) * 128], w2a2[0:64, fs], self.lora[0:64, 0, ct], True, True, ["w2a2"], [kLO[hp // 4]])
                MM(self.psS[:, 1024 + hp * 128:1024 + (hp + 1) * 128], w2a2[64:128, fs], self.lora[64:128, 0, ct], True, True, ["w2a2"], [kHI[hp // 4]])
            TT("dve", b0, ps_lo, bl(C_W0), ALU.add, kLO + ["pc"], ["b0"])
            ACT(b0, b0, AF.Exp, ["b0"], ["b0"], scale=-1.0)
            ACT(b0, b0, AF.Ln, ["b0"], ["b0"], bias=1.0)
            ACT(b1, b0, AF.Exp, ["b0", "pc"], ["b1"], scale=-1.0, bias=self.col(C_MHALF))
            for hp in range(8):
                P.op("dve", lambda e, hp=hp: e.tensor_tensor_scan(out=b2[:, hp, :], data0=b1[:, hp, :], data1=self.onesF[:, 0:128], initial=0.0, op0=ALU.add, op1=ALU.mult),
                     r=["b1", "onesF"], w=["b2"])
            ACT(b3, b2, AF.Exp, ["b2"], ["b3"], scale=-1.0)
            ACT(b4, b2, AF.Exp, ["b2"], ["b4"])
            TT("pool", b0, b2, b1, ALU.subtract, ["b1", "b2"], ["b0"])
            ACT(b5, b0, AF.Exp, ["b0"], ["b5"], scale=-1.0)
            TT("dve", b0, ps_hi, bl(C_A0), ALU.add, kHI + ["pc"], ["b0"])
            ACT(b0, b0, AF.Exp, ["b0"], ["b0"], scale=-1.0)
            P.op("dve", lambda e: e.tensor_scalar(out=b0, in0=b0, scalar1=1.0, scalar2=None, op0=ALU.add), r=["b0"], w=["b0"])
            P.op("dve", lambda e: e.reciprocal(out=b0, in_=b0), r=["b0"], w=["b0"])
            TT("dve", b1, k_, bl(C_KK), ALU.mult, ["pc"], ["b1"])
            TT("pool", SQ, b1, b1, ALU.mult, ["b1"], ["SQ"])
            for hp in range(8):
                MM(self.psS[:, hp * 128:(hp + 1) * 128], self.onesbdB, SQ[:, hp, :], True, True, ["SQ", "cB"], [kLO[hp // 4]])
            P.op("dve", lambda e: e.tensor_scalar(out=b2, in0=ps_lo, scalar1=1e-24, scalar2=None, op0=ALU.max), r=kLO, w=["b2"])
            ACT(b2, b2, AF.Ln, ["b2"], ["b2"])
            ACT(b2, b2, AF.Exp, ["b2"], ["b2"], scale=-0.5)
            TT("dve", b1, b1, b2, ALU.mult, ["b1", "b2"], ["b1"])
            STT(b5, b1, -1.0, b5, ALU.mult, ALU.mult, ["b1", "b5"], ["b5"])
            TT("pool", b2, b1, b0, ALU.mult, ["b1", "b0"], ["b2"])
            TT("dve", b2, b2, b4, ALU.mult, ["b2", "b4"], ["b2"])
            STT(b0, b0, -1.0, bl(C_KA), ALU.add, ALU.mult, ["b0", "pc"], ["b0"])
            STT(b0, b0, 1.0, k_, ALU.add, ALU.mult, ["b0"], ["b0"])
            TT("dve", b1, r_, b0, ALU.mult, ["b0", "b1"], ["b1"])
            TT("dve", SQ, b1, bl(C_RK), ALU.mult, ["b1", "pc"], ["SQ"])
            for hp in range(8):
                MM(self.psS[:, 1024 + hp * 128:1024 + (hp + 1) * 128], self.onesbdB, SQ[:, hp, :], True, True, ["SQ", "cB"], [kHI[hp // 4]])
            TT("dve", BON, ps_hi, v_, ALU.mult, kHI, ["BON"])
            TT("pool", b0, b0, b4, ALU.mult, ["b0", "b4"], ["b0"])
            TT("dve", b1, r_, b3, ALU.mult, ["b3"], ["b1"])
            Kt_, Qt_, Bt_, Pd_, At_ = b0, b1, b2, b3, b5
            if c == 0:
                for nm, bb, kk_ in (("Kt", b0, "b0"), ("Qt", b1, "b1"), ("Bt", b2, "b2"), ("Pd", b3, "b3"), ("iP", b4, "b4"), ("At", b5, "b5")):
                    self.tap("%s%d" % (nm, hf), bb, [128, 8, 128], [kk_])
            bk4 = self.bank[4].bitcast(BF16)
            for hp in range(8):
                P.op("pe", lambda e, hp=hp, ct=ct: e.transpose(out=bk4[:, hp * 128:(hp + 1) * 128], in_=self.rkv[:, 16 + hp, ct], identity=self.identB), r=["cB"], w=[self.bkey[4]])
            P.op("act", lambda e: e.copy(out=VT, in_=bk4), r=[self.bkey[4]], w=["VT"])
            for hp in range(8):
                P.op("pe", lambda e, hp=hp: e.transpose(out=self.psS[:, hp * 128:(hp + 1) * 128], in_=Kt_[:, hp, :], identity=self.identF), r=["b0", "cF"], w=[kLO[hp // 4]])
                P.op("pe", lambda e, hp=hp: e.transpose(out=self.psS[:, 1024 + hp * 128:1024 + (hp + 1) * 128], in_=Bt_[:, hp, :], identity=self.identF), r=["b2", "cF"], w=[kHI[hp // 4]])
            P.op("act", lambda e: e.copy(out=KtT, in_=self.psS[:, 0:1024]), r=kLO, w=["KtT"])
            P.op("dve", lambda e: e.tensor_copy(out=BtT, in_=self.psS[:, 1024:2048]), r=kHI, w=["BtT"])
            for hg in range(4):
                hd = [(2 * hg + q // 2, (q % 2) * 64) for q in range(4)]
                qs = lambda q: slice(q * 128, (q + 1) * 128)
                for q, (pr, hs) in enumerate(hd):
                    sl = slice(hs, hs + 64)
                    MM(self.bank[0][:, qs(q)], Bt_[sl, pr, :], At_[sl, pr, :], True, True, ["b2", "b5"], [self.bkey[0]])
                    MM(self.bank[1][:, qs(q)], At_[sl, pr, :], Bt_[sl, pr, :], True, True, ["b2", "b5"], [self.bkey[1]])
                    MM(self.bank[2][:, qs(q)], Kt_[sl, pr, :], At_[sl, pr, :], True, True, ["b0", "b5"], [self.bkey[2]])
                    MM(self.bank[3][:, qs(q)], Bt_[sl, pr, :], Qt_[sl, pr, :], True, True, ["b2", "b1"], [self.bkey[3]])
                    MM(self.bank[4][:, qs(q)], Kt_[sl, pr, :], Qt_[sl, pr, :], True, True, ["b0", "b1"], [self.bkey[4]])
                TT("dve", v4(Nn), v4(self.bank[0]), bm(self.iu, 4), ALU.mult, [self.bkey[0], "cF"], ["Nn"])
                TT("dve", v4(NnT), v4(self.bank[1]), bm(self.il, 4), ALU.mult, [self.bkey[1], "cF"], ["NnT"])
                TT("dve", v4(Makm), v4(self.bank[2]), bm(self.iu, 4), ALU.mult, [self.bkey[2], "cF"], ["Makm"])
                TT("dve", v4(Nbqm), v4(self.bank[3]), bm(self.iue, 4), ALU.mult, [self.bkey[3], "cF"], ["Nbqm"])
                TT("dve", v4(Nkqm), v4(self.bank[4]), bm(self.iue, 4), ALU.mult, [self.bkey[4], "cF"], ["Nkqm"])
                TT("pool", v4(Tm), v4(Nn), bm(self.identF, 4), ALU.add, ["Nn", "cF"], ["Tm"])
                for lvl in range(6):
                    last = lvl == 5
                    for q in range(4):
                        if not last:
                            MM(self.bank[0][:, qs(q)], NnT[:, qs(q)], Nn[:, qs(q)], True, True, ["Nn", "NnT"], [self.bkey[0]])
                        MM(self.bank[1][:, qs(q)], Nn[:, qs(q)], NnT[:, qs(q)], True, True, ["Nn", "NnT"], [self.bkey[1]])
                    TT("dve", v4(Wm), v4(self.bank[1]), bm(self.identF, 4), ALU.add, [self.bkey[1], "cF"], ["Wm"])
                    if not last:
                        P.op("act", lambda e: e.copy(out=NnT, in_=self.bank[1]), r=[self.bkey[1]], w=["NnT"])
                        P.op("act", lambda e: e.copy(out=Nn, in_=self.bank[0]), r=[self.bkey[0]], w=["Nn"])
                    for q in range(4):
                        MM(self.bank[2][:, qs(q)], Wm[:, qs(q)], Tm[:, qs(q)], True, True, ["Wm", "Tm"], [self.bkey[2]])
                    self.copy(self.evac_eng(), Tm, self.bank[2], r=[self.bkey[2]], w=["Tm"])
                if c == 0 and hg == 0:
                    self.tap("Tm%d" % hf, Tm, [128, 512], ["Tm"])
                    self.tap("Makm%d" % hf, Makm, [128, 512], ["Makm"])
                    self.tap("Nbqm%d" % hf, Nbqm, [128, 512], ["Nbqm"])
                q64 = lambda q: slice(q * 64, (q + 1) * 64)
                for q, (pr, hs) in enumerate(hd):
                    sl = slice(hs, hs + 64)
                    MM(self.bank[5][:, q64(q)], At_[sl, pr, :], self.STp[sl, pr, hs:hs + 64], True, False, ["b5", ("STp", pr)], [self.bkey[5]])
                    MM(self.bank[5][:, q64(q)], Makm[:, qs(q)], VT[:, pr * 128 + hs:pr * 128 + hs + 64], False, True, ["Makm", "VT"], [self.bkey[5]])
                P.op("act", lambda e: e.copy(out=ZTs, in_=self.bank[5][:, 0:256]), r=[self.bkey[5]], w=["ZTs"])
                for q in range(4):
                    MM(self.bank[5][:, 256 + q * 64:256 + (q + 1) * 64], Tm[:, qs(q)], ZTs[:, q64(q)], True, True, ["Tm", "ZTs"], [self.bkey[5]])
                P.op("act", lambda e: e.copy(out=UTs, in_=self.bank[5][:, 256:512]), r=[self.bkey[5]], w=["UTs"])
                for q, (pr, hs) in enumerate(hd):
                    sl = slice(hs, hs + 64)
                    MM(self.bank[6][:, q64(q)], Qt_[sl, pr, :], self.STp[sl, pr, hs:hs + 64], True, False, ["b1", ("STp", pr)], [self.bkey[6]])
                    MM(self.bank[6][:, q64(q)], Nbqm[:, qs(q)], UTs[:, q64(q)], False, False, ["Nbqm", "UTs"], [self.bkey[6]])
                    MM(self.bank[6][:, q64(q)], Nkqm[:, qs(q)], VT[:, pr * 128 + hs:pr * 128 + hs + 64], False, True, ["Nkqm", "VT"], [self.bkey[6]])
                P.op("dve", lambda e, hg=hg: e.tensor_copy(out=Ytm[:, hg * 256:(hg + 1) * 256], in_=self.bank[6][:, 0:256]), r=[self.bkey[6]], w=[("Ytm", hg)])
                for pl in range(2):
                    pr = 2 * hg + pl
                    fs = slice(pr * 128, (pr + 1) * 128)
                    ob = self.bank[7][:, pl * 128:(pl + 1) * 128]
                    MM(ob, self.identF, self.STp[:, pr, :], True, False, ["cF", ("STp", pr)], [self.bkey[7]])
                    MM(ob, BtT[:, fs], UTs[:, pl * 128:(pl + 1) * 128], False, False, ["BtT", "UTs"], [self.bkey[7]])
                    MM(ob, KtT[:, fs], VT[:, fs], False, True, ["KtT", "VT"], [self.bkey[7]])
                for pl in range(2):
                    pr = 2 * hg + pl
                    for hs in (0, 64):
                        sl = slice(hs, hs + 64)
                        P.op("dve", lambda e, pr=pr, pl=pl, hs=hs, sl=sl: e.tensor_scalar(out=self.STp[sl, pr, hs:hs + 64], in0=self.bank[7][sl, pl * 128 + hs:pl * 128 + hs + 64],
                                                                                        scalar1=Pd_[sl, pr, 127:128], scalar2=None, op0=ALU.mult),
                             r=[self.bkey[7], "b3"], w=[("STp", pr)])
            Yk = [("Ytm", hg) for hg in range(4)]
            if c == 1:
                self.tap("Ytmc1_%d" % hf, Ytm, [128, 1024], Yk)
            if c == 0:
                self.tap("STp%d" % hf, self.STp[:], [128, 8, 128], [("STp", p_) for p_ in range(8)])
                self.tap("Ytm%d" % hf, Ytm, [128, 1024], Yk)
                self.tap("VT%d" % hf, VT, [128, 1024], ["VT"])
                self.tap("KtT%d" % hf, KtT, [128, 1024], ["KtT"])
            Y3 = Ytm.rearrange("p (h n) -> p h n", n=64)
            G1f = G1.rearrange("p a b -> p (a b)")
            P.op("dve", lambda e: e.tensor_reduce(out=gs[:, 0:16], in_=Y3, axis=AX.X, op=ALU.add), r=Yk, w=["gs0"])
            TT("pool", G1f, Ytm, Ytm, ALU.mult, Yk, ["G1"])
            P.op("dve", lambda e: e.tensor_reduce(out=gs[:, 16:32], in_=G1f.rearrange("p (h n) -> p h n", n=64), axis=AX.X, op=ALU.add), r=["G1"], w=["gs1"])
            P.op("dve", lambda e: e.tensor_scalar(out=gs[:, 32:48], in0=gs[:, 0:16], scalar1=1.0 / 64, scalar2=None, op0=ALU.mult), r=["gs0"], w=["gs2"])
            TT("dve", gs[:, 48:64], gs[:, 32:48], gs[:, 32:48], ALU.mult, ["gs2"], ["gs3"])
            STT(gs[:, 64:80], gs[:, 16:32], 1.0 / 64, gs[:, 48:64], ALU.mult, ALU.subtract, ["gs1", "gs3"], ["gs4"])
            ACT(gs[:, 80:96], gs[:, 64:80], AF.Ln, ["gs4", "pc"], ["gs5"], bias=self.col(C_GNEPS))
            ACT(gs[:, 96:112], gs[:, 80:96], AF.Exp, ["gs5"], ["gs6"], scale=-0.5)
            TT("dve", Y3, Y3, gs[:, 32:48].unsqueeze(2).to_broadcast([128, 16, 64]), ALU.subtract, Yk + ["gs2"], Yk)
            TT("dve", Y3, Y3, gs[:, 96:112].unsqueeze(2).to_broadcast([128, 16, 64]), ALU.mult, Yk + ["gs6"], Yk)
            for hp in range(8):
                fs = slice(hp * 128, (hp + 1) * 128)
                P.op("pe", lambda e, hp=hp, fs=fs: e.transpose(out=self.psS[:, fs], in_=Ytm[:, fs], identity=self.identF), r=Yk + ["cF"], w=[kLO[hp // 4]])
                MM(self.psS[:, 1024 + hp * 128:1024 + (hp + 1) * 128], g2a[:, fs], self.lora[:, 1, ct], True, False, ["g2a"], [kHI[hp // 4]])
                MM(self.psS[:, 1024 + hp * 128:1024 + (hp + 1) * 128], g2b[0:32, fs], self.lora[0:32, 2, ct], False, True, ["g2b"], [kHI[hp // 4]])
            TT("dve", G1, ps_lo, bl(C_GNW), ALU.mult, kLO + ["pc", "G1"], ["G1"])
            TT("dve", G1, G1, bl(C_GNB), ALU.add, ["G1", "pc"], ["G1"])
            TT("dve", G1, G1, BON, ALU.add, ["G1", "BON"], ["G1"])
            if c == 1:
                self.tap("gsc1_%d" % hf, gs, [128, 128], ["gs0", "gs1", "gs2", "gs3", "gs4", "gs5", "gs6"])
                self.tap("G1c1_%d" % hf, G1, [128, 8, 128], ["G1"])
                self.tap("BONc1_%d" % hf, BON, [128, 8, 128], ["BON"])
                self.tap("Ync1_%d" % hf, Ytm, [128, 1024], Yk)
            if c == 0:
                self.tap("gs%d" % hf, gs, [128, 128], ["gs0", "gs1", "gs2", "gs3", "gs4", "gs5", "gs6"])
                self.tap("G1_%d" % hf, G1, [128, 8, 128], ["G1"])
                self.tap("Yn%d" % hf, Ytm, [128, 1024], Yk)
            TT("dve", self.y[:, 0:8, ct], G1, ps_hi, ALU.mult, ["G1"] + kHI, [("y", hp) for hp in range(8)])
        self.tap("yrw%d" % hf, self.y[:, 0:8, :], [128, 8, T], [("y", hp) for hp in range(8)])

    def ffn_block(self, hf):
        P = self.P
        x1 = self.cv(32, [128, NTT, D], F32)
        wo = [self.cv(96 + 16 * i, [128, DC, 512], BF16) for i in range(2)]
        xin = [self.cv(128 + 2 * i, [128, 512], F32) for i in range(2)]
        ykeys = [("y", c) for c in range(16)]
        for ds in range(4):
            sl = ds % 2
            self.loadw(wo[sl], self.w_out, 0, ds * 512, 512, DC, [("wo", sl)], "wo%d" % sl, step=2)
            for tt in range(NTT):
                bi = self.next_bank()
                xs = (ds * NTT + tt) % 2
                self.load(xin[xs], self.x[hf * T + tt * 128:hf * T + (tt + 1) * 128, ds * 512:(ds + 1) * 512], [("xin", xs)], "xin%d" % xs)
                for kc in range(DC):
                    P.op("pe", lambda e, bi=bi, kc=kc, sl=sl, tt=tt: e.matmul(self.bank[bi], lhsT=self.y[:, kc, tt * 128:(tt + 1) * 128], rhs=wo[sl][:, kc, :], start=(kc == 0), stop=(kc == DC - 1)),
                         r=[("wo", sl)] + ykeys, w=[self.bkey[bi]])
                P.op("dve", lambda e, bi=bi, tt=tt, ds=ds, xs=xs: e.tensor_tensor(out=x1[:, tt, ds * 512:(ds + 1) * 512], in0=self.bank[bi], in1=xin[xs], op=ALU.add),
                     r=[self.bkey[bi], ("xin", xs)], w=[("x1", tt, ds)])
        P.barrier()
        P.mark("h%d w_out done" % hf)
        self.tap("x1_%d" % hf, x1, [128, NTT, D], [])
        if self.stage <= 5:
            return
        h2T = self.cv(0, [128, DC, T], BF16)
        self.norm_transpose(lambda tt: (x1[:, tt, :], []), 1, h2T, "F", 144)
        P.barrier()
        P.mark("h%d ffn norm done" % hf)
        wg = [self.cv(96 + 8 * i, [128, DC, 256], BF16) for i in range(2)]
        wu = [self.cv(112 + 8 * i, [128, DC, 256], BF16) for i in range(2)]
        wd = [self.cv(128 + 8 * i, [128, 2, D], BF16) for i in range(3)]
        actbs = [self.cv(152 + 4 * i, [128, 2, T], BF16) for i in range(2)]
        gbuf = self.cv(160, [128, 576], F32)
        cscr = self.cv(162.25, [128, 512], F32)
        NG = NFC // 2

        def down(g):
            sl3 = g % 3
            actb = actbs[g % 2]
            for tt in range(NTT):
                for ds in range(4):
                    bi = self.next_bank()
                    for j in range(2):
                        P.op("pe", lambda e, bi=bi, j=j, tt=tt, ds=ds, sl3=sl3, actb=actb: e.matmul(self.bank[bi], lhsT=actb[:, j, tt * 128:(tt + 1) * 128], rhs=wd[sl3][:, j, ds * 512:(ds + 1) * 512], start=(j == 0), stop=(j == 1)),
                             r=[("wd", sl3), ("actb", g % 2, j)], w=[self.bkey[bi]])
                    d = x1[:, tt, ds * 512:(ds + 1) * 512]
                    P.op("dve", lambda e, bi=bi, d=d: e.tensor_tensor(out=d, in0=self.bank[bi], in1=d, op=ALU.add), r=[self.bkey[bi]], w=[("x1", tt, ds)])

        for gi in range(NG):
            sl = gi % 2
            sl3 = gi % 3
            wk = ("wgu", sl)
            c0 = gi * 256
            actb = actbs[gi % 2]
            self.loadw(wg[sl], self.w_gate, 0, c0, 256, DC, [wk], "wg%d" % sl, step=4)
            self.loadw(wu[sl], self.w_up, 0, c0, 256, DC, [wk], "wu%d" % sl, step=4)
            for j in range(2):
                for q in range(2):
                    self.load(wd[sl3][:, j, q * 1024:(q + 1) * 1024], self.w_down[c0 + j * 128:c0 + (j + 1) * 128, q * 1024:(q + 1) * 1024], [("wd", sl3)], "wd%d" % sl3, eng="pool")
            for j in range(2):
                fc = gi * 2 + j
                cw = C_CW + fc * 3
                for tg in range(2):
                    t0 = tg * 512
                    bG = self.next_bank()
                    for kc in range(DC):
                        P.op("pe", lambda e, bG=bG, kc=kc, sl=sl, j=j, t0=t0: e.matmul(self.bank[bG], lhsT=wg[sl][:, kc, j * 128:(j + 1) * 128], rhs=h2T[:, kc, t0:t0 + 512], start=(kc == 0), stop=(kc == DC - 1)), r=[wk], w=[self.bkey[bG]])
                    bU = self.next_bank()
                    for kc in range(DC):
                        P.op("pe", lambda e, bU=bU, kc=kc, sl=sl, j=j, t0=t0: e.matmul(self.bank[bU], lhsT=wu[sl][:, kc, j * 128:(j + 1) * 128], rhs=h2T[:, kc, t0:t0 + 512], start=(kc == 0), stop=(kc == DC - 1)), r=[wk], w=[self.bkey[bU]])
                    P.op("act", lambda e, bG=bG: e.copy(out=gbuf[:, 2:514], in_=self.bank[bG]), r=[self.bkey[bG]], w=["gbuf"])
                    P.op("act", lambda e, fc=fc: e.copy(out=gbuf[:, 0:2], in_=self.gcarry[:, fc, :]), r=[("gc", fc), "gbuf"], w=["gbuf"])
                    P.op("dve", lambda e, cw=cw, fc=fc: e.tensor_scalar(out=cscr, in0=gbuf[:, 2:514], scalar1=self.col(cw + 2), scalar2=self.col(C_CB + fc), op0=ALU.mult, op1=ALU.add), r=["gbuf", "pc"], w=["cscr"])
                    P.op("dve", lambda e, cw=cw: e.scalar_tensor_tensor(out=cscr, in0=gbuf[:, 1:513], scalar=self.col(cw + 1), in1=cscr, op0=ALU.mult, op1=ALU.add), r=["gbuf", "cscr", "pc"], w=["cscr"])
                    P.op("dve", lambda e, cw=cw: e.scalar_tensor_tensor(out=cscr, in0=gbuf[:, 0:512], scalar=self.col(cw), in1=cscr, op0=ALU.mult, op1=ALU.add), r=["gbuf", "cscr", "pc"], w=["cscr"])
                    P.op("act", lambda e, fc=fc: e.copy(out=self.gcarry[:, fc, :], in_=gbuf[:, 512:514]), r=["gbuf"], w=[("gc", fc)])
                    P.op("act", lambda e: e.activation(out=cscr, in_=cscr, func=AF.Silu), r=["cscr"], w=["cscr"])
                    P.op("dve", lambda e, bU=bU, j=j, t0=t0, actb=actb: e.tensor_tensor(out=actb[:, j, t0:t0 + 512], in0=self.bank[bU], in1=cscr, op=ALU.mult), r=[self.bkey[bU], "cscr"], w=[("actb", gi % 2, j)])
            if gi >= 1:
                down(gi - 1)
        down(NG - 1)
        P.barrier()
        P.mark("h%d ffn main done" % hf)
        self.load(self.gA[:], self.gbd[2], ["gA"], "gA")
        ot = [self.cv(96 + 8 * i, [128, D], F32) for i in range(2)]
        junk = self.cv(112, [128, D], BF16)
        st_ = self.cv(116, [128, 4 * NTT], F32)
        P.op("dve", lambda e: e.memset(st_, 0.0), w=["fstat"])
        for tt in range(NTT):
            c = 4 * tt
            o_ = ot[tt % 2]
            ok = ("ot", tt % 2)
            P.op("act", lambda e, tt=tt, c=c: e.activation(out=junk, in_=x1[:, tt, :], func=AF.Square, scale=float(D) ** -0.5, accum_out=st_[:, c:c + 1]), r=["fstat"], w=["fjunk", ("fs", tt)])
            P.op("act", lambda e, c=c: e.activation(out=st_[:, c + 2:c + 3], in_=st_[:, c:c + 1], func=AF.Sqrt, bias=self.col(C_EPS), scale=1.0), r=[("fs", tt), "pc"], w=[("fs2", tt)])
            P.op("dve", lambda e, c=c: e.reciprocal(out=st_[:, c + 3:c + 4], in_=st_[:, c + 2:c + 3]), r=[("fs2", tt)], w=[("fs3", tt)])
            P.op("dve", lambda e, tt=tt, c=c, o_=o_: e.scalar_tensor_tensor(out=o_, in0=x1[:, tt, :], scalar=st_[:, c + 3:c + 4], in1=self.gA[:], op0=ALU.mult, op1=ALU.mult), r=[("fs3", tt), "gA"], w=[ok])
            r0 = hf * T + tt * 128
            for q in range(4):
                P.op("sp", lambda e, o_=o_, r0=r0, q=q: e.dma_start(out=self.out[r0:r0 + 128, q * 512:(q + 1) * 512], in_=o_[:, q * 512:(q + 1) * 512]), r=[ok], dma="out%d" % (tt % 2))


def _cols(v, n):
    v = np.asarray(v, np.float32).reshape(-1)
    pad = n * 128 - v.shape[0]
    if pad:
        v = np.concatenate([v, np.zeros(pad, np.float32)])
    return np.ascontiguousarray(v.reshape(n, 128).T)


def host_consts():
    ident = np.eye(128, dtype=np.float32)
    s = np.arange(128)
    iu = (s[:, None] < s[None, :]).astype(np.float32)
    iue = (s[:, None] <= s[None, :]).astype(np.float32)
    il = iu.T.copy()
    obd = np.zeros((128, 128), np.float32)
    obd[:64, :64] = 1.0
    obd[64:, 64:] = 1.0
    cmask = np.where(s[None, :] <= s[:, None], 0.0, -30000.0).astype(np.float32)
    sel2 = np.tile(np.eye(64, dtype=np.float32), (2, 2))
    return np.ascontiguousarray(np.concatenate([ident, iu, iue, il, obd, cmask, sel2], axis=1))


def make_in_maps(inp):
    f = lambda a: np.ascontiguousarray(np.asarray(a, np.float32))
    half = 32
    invf = (10000.0 ** (-(np.arange(half, dtype=np.float32)) / np.float32(half))).astype(np.float32)
    invf2 = np.concatenate([invf, invf, invf, invf])[:, None]
    offc = np.concatenate([np.full(64, math.pi / 2, np.float32), np.zeros(64, np.float32)])[:, None]
    convw = f(inp["ffn_conv_w"][0])
    convw_c = np.ascontiguousarray(convw.T.reshape(NFC, 128, 3).transpose(1, 0, 2).reshape(128, NFC * 3))
    pcols = np.concatenate([
        _cols(inp["rwkv_mu"][0], 27), _cols(inp["rwkv_w0"][0], 8), _cols(inp["rwkv_a0"][0], 8), _cols(inp["rwkv_k_k"][0], 8),
        _cols(inp["rwkv_k_a"][0], 8), _cols(inp["rwkv_r_k"][0], 8), _cols(inp["rwkv_gn_w"][0], 8), _cols(inp["rwkv_gn_b"][0], 8),
        _cols(inp["mla_q_norm_g"][0], 4), _cols(inp["mla_kv_norm_g"][0], 4), _cols(inp["ffn_conv_b"][0], NFC), convw_c,
        invf2.astype(np.float32), offc.astype(np.float32)], axis=1)
    assert pcols.shape == (128, 269), pcols.shape
    gains = np.stack([np.broadcast_to(f(inp["attn_norm_g"][0])[None, :], (128, D)),
                      np.broadcast_to(f(inp["ffn_norm_g"][0])[None, :], (128, D)),
                      np.broadcast_to(f(inp["final_norm_g"])[None, :], (128, D))])
    shared = {
        "c_f32": host_consts(), "pcols": np.ascontiguousarray(pcols), "gains_b": np.ascontiguousarray(gains),
        "w2a2": np.ascontiguousarray(np.concatenate([f(inp["rwkv_w2"][0]), f(inp["rwkv_a2"][0])], axis=0)),
        "g2": f(inp["rwkv_g2"][0]),
        "w_in": f(inp["w_in"][0]), "w_uq": f(inp["mla_w_uq"][0]), "w_ukv": f(inp["mla_w_ukv"][0]), "w_out": f(inp["w_out"][0]),
        "w_gate": f(inp["ffn_w_gate"][0]), "w_up": f(inp["ffn_w_up"][0]), "w_down": f(inp["ffn_w_down"][0]),
    }
    maps = []
    xs = f(inp["x"])
    pos = np.asarray(inp["positions"], np.int32)
    for b in range(xs.shape[0]):
        m = dict(shared)
        m["x"] = xs[b]
        m["pos_b"] = np.ascontiguousarray(np.broadcast_to(pos[b][None, :], (128, S)))
        maps.append(m)
    return maps


_NC_CACHE = {}


def kernel(**inputs):
    if "nc" not in _NC_CACHE:
        kb = K()
        _NC_CACHE["nc"] = kb.build()
        _NC_CACHE["names"] = set(kb.din.keys())
    nc = _NC_CACHE["nc"]
    maps = make_in_maps(inputs)
    maps = [{k: v for k, v in m.items() if k in _NC_CACHE["names"]} for m in maps]
    res = run_bass_kernel_spmd(nc, maps, core_ids=list(range(len(maps))))
    return np.stack([np.asarray(r["out"], np.float32) for r in res.results], axis=0)
```
